# Optimizing a Trainium2 kernel written in Bass

```python
import math
import jax, jax.numpy as jnp
from jax import lax
import numpy as np

D_MODEL = 1024
BATCH = 2
SEQ = 8192
DEPTH = 1

HEAD_DIM = 64
DIL_GROUPS = ((128, 1), (512, 4), (2048, 16))
N_GROUPS = 3
HEADS_PER_GROUP = 4
N_ATTN_HEADS = N_GROUPS * HEADS_PER_GROUP
ATTN_W = N_ATTN_HEADS * HEAD_DIM
ATTN_OUT_W = HEADS_PER_GROUP * HEAD_DIM
BLK = 128
SGU_CHUNK = 128
SGU_GROUPS = 4
SGU_W = 512
SGU_GROUP_W = SGU_W // SGU_GROUPS
N_BRANCH = 2
IN_W = 3 * ATTN_W + 2 * SGU_W + N_BRANCH * D_MODEL
MEM_LEN = 256
MEM_HEADS = 4
MEM_HEAD_DIM = 128
MEM_W = MEM_HEADS * MEM_HEAD_DIM
D_FF = -(-8 * D_MODEL // (3 * 256)) * 256
EPS = 1e-6

kernel_name = "hybrid_dilated_attn_sgu_gated_block"


def rmsnorm(x, g):
    xf = x.astype(jnp.float32)
    r = lax.rsqrt(jnp.mean(xf * xf, axis=-1, keepdims=True) + EPS)
    return (xf * r * g.astype(jnp.float32)).astype(x.dtype)


def alibi_slopes_grouped():
    def pow2(n):
        start = 2.0 ** (-8.0 / n)
        return [start ** (i + 1) for i in range(n)]
    n = N_ATTN_HEADS
    if math.log2(n).is_integer():
        s = pow2(n)
    else:
        c = 2 ** int(math.floor(math.log2(n)))
        s = pow2(c) + pow2(2 * c)[0::2][: n - c]
    s = np.array(sorted(s, reverse=True), dtype=np.float32)
    return s.reshape(N_GROUPS, HEADS_PER_GROUP)


def dilated_causal_window_attention(q, k, v, slopes, window, dilation):
    B, S, H, Dh = q.shape
    n_back = window // dilation
    assert n_back <= BLK
    sub_len = -(-S // dilation)
    L = -(-sub_len // BLK) * BLK
    S_pad = L * dilation
    nb = L // BLK

    def to_blocks(t):
        t = jnp.pad(t, ((0, 0), (0, S_pad - S), (0, 0), (0, 0)))
        t = t.reshape(B, nb, BLK, dilation, H, Dh)
        return t.transpose(0, 3, 4, 1, 2, 5)

    def with_prev(t):
        prev = jnp.pad(t, ((0, 0), (0, 0), (0, 0), (1, 0), (0, 0), (0, 0)))[:, :, :, :-1]
        return jnp.concatenate([prev, t], axis=4)

    qb = to_blocks(q)
    kk = with_prev(to_blocks(k))
    vv = with_prev(to_blocks(v))

    s = jnp.einsum('brhnqd,brhnkd->brhnqk', qb, kk).astype(jnp.float32) * (Dh ** -0.5)
    steps = (np.arange(BLK)[:, None] + BLK) - np.arange(2 * BLK)[None, :]
    band = (steps >= 0) & (steps <= n_back)
    no_prev = (np.arange(nb)[:, None, None] == 0) & (np.arange(2 * BLK)[None, None, :] < BLK)
    valid = band[None] & ~no_prev
    dist = jnp.asarray((np.clip(steps, 0, None) * dilation).astype(np.float32))
    bias = -jnp.asarray(slopes)[:, None, None, None] * dist[None, None]
    s = jnp.where(jnp.asarray(valid), s + bias, -jnp.inf)
    mx = jnp.max(s, axis=-1, keepdims=True)
    e = jnp.exp(s - mx)
    den = jnp.sum(e, axis=-1, keepdims=True)
    o = jnp.einsum('brhnqk,brhnkd->brhnqd', e, vv.astype(jnp.float32)) / den
    lse = (mx + jnp.log(den))[..., 0]
    o = o.transpose(0, 3, 4, 1, 2, 5).reshape(B, S_pad, H, Dh)[:, :S]
    lse = lse.transpose(0, 3, 4, 1, 2).reshape(B, S_pad, H)[:, :S]
    return o, lse


def spatial_gating(uv, w_s, b_s, g):
    B, S, _ = uv.shape
    z = jax.nn.gelu(uv)
    u, v = jnp.split(z, 2, axis=-1)
    v = rmsnorm(v, g)
    v = v.reshape(B, S // SGU_CHUNK, SGU_CHUNK, SGU_GROUPS, SGU_GROUP_W)
    mixed = jnp.einsum('gts,bnsgc->bntgc', jnp.tril(w_s), v) + b_s.T[:, :, None]
    return u * mixed.reshape(B, S, SGU_W)


def memory_cross_attention(c, m, w_q, w_kv, w_o):
    B, S, _ = c.shape
    M = m.shape[1]
    q = (c @ w_q).reshape(B, S, MEM_HEADS, MEM_HEAD_DIM)
    kv = (m @ w_kv).reshape(B, M, 2, MEM_HEADS, MEM_HEAD_DIM)
    k, v = kv[:, :, 0], kv[:, :, 1]
    s = jnp.einsum('bshd,bmhd->bhsm', q, k).astype(jnp.float32) * (MEM_HEAD_DIM ** -0.5)
    p = jax.nn.softmax(s, axis=-1)
    o = jnp.einsum('bhsm,bmhd->bshd', p, v.astype(jnp.float32)).astype(c.dtype)
    return o.reshape(B, S, MEM_W) @ w_o


def setup_inputs(seed: int = 0) -> dict:
    key = jax.random.key(seed)
    ks = jax.random.split(key, 24)
    f32 = jnp.float32

    def nrm(k, shape, scale):
        return jax.random.normal(k, shape, f32) * scale

    def gain(k, shape):
        return 1.0 + 0.02 * jax.random.normal(k, shape, f32)

    L = DEPTH
    return {
        "x": nrm(ks[0], (BATCH, SEQ, D_MODEL), 1.0),
        "mem": nrm(ks[1], (BATCH, MEM_LEN, D_MODEL), 1.0),
        "g_mix": gain(ks[2], (L, D_MODEL)),
        "w_in": nrm(ks[3], (L, D_MODEL, IN_W), D_MODEL ** -0.5),
        "b_gate": nrm(ks[4], (L, N_BRANCH * D_MODEL), 0.1),
        "w_sgu_spatial": nrm(ks[5], (L, SGU_GROUPS, SGU_CHUNK, SGU_CHUNK), 0.5 * SGU_CHUNK ** -0.5),
        "b_sgu_spatial": 1.0 + nrm(ks[6], (L, SGU_GROUPS, SGU_CHUNK), 0.1),
        "g_sgu": gain(ks[7], (L, SGU_W)),
        "w_branch_attn": nrm(ks[8], (L, ATTN_OUT_W, D_MODEL), ATTN_OUT_W ** -0.5),
        "w_branch_sgu": nrm(ks[9], (L, SGU_W, D_MODEL), SGU_W ** -0.5),
        "w_out": nrm(ks[10], (L, D_MODEL, D_MODEL), D_MODEL ** -0.5),
        "g_cross": gain(ks[11], (L, D_MODEL)),
        "g_mem": gain(ks[12], (L, D_MODEL)),
        "w_q_cross": nrm(ks[13], (L, D_MODEL, MEM_W), D_MODEL ** -0.5),
        "w_kv_cross": nrm(ks[14], (L, D_MODEL, 2 * MEM_W), D_MODEL ** -0.5),
        "w_o_cross": nrm(ks[15], (L, MEM_W, D_MODEL), MEM_W ** -0.5),
        "g_ffn": gain(ks[16], (L, D_MODEL)),
        "w_gate_up": nrm(ks[17], (L, D_MODEL, 2 * D_FF), D_MODEL ** -0.5),
        "w_down": nrm(ks[18], (L, D_FF, D_MODEL), D_FF ** -0.5),
        "g_final": gain(ks[19], (D_MODEL,)),
    }


def reference(x, mem, g_mix, w_in, b_gate, w_sgu_spatial, b_sgu_spatial, g_sgu,
              w_branch_attn, w_branch_sgu, w_out, g_cross, g_mem, w_q_cross,
              w_kv_cross, w_o_cross, g_ffn, w_gate_up, w_down, g_final):
    B, S, D = x.shape
    slopes = alibi_slopes_grouped()
    h = x
    for l in range(DEPTH):
        a = rmsnorm(h, g_mix[l])
        proj = a @ w_in[l]
        q, k, v, uv, gl = jnp.split(
            proj, [ATTN_W, 2 * ATTN_W, 3 * ATTN_W, 3 * ATTN_W + 2 * SGU_W], axis=-1)
        q = q.reshape(B, S, N_GROUPS, HEADS_PER_GROUP, HEAD_DIM)
        k = k.reshape(B, S, N_GROUPS, HEADS_PER_GROUP, HEAD_DIM)
        v = v.reshape(B, S, N_GROUPS, HEADS_PER_GROUP, HEAD_DIM)
        outs, lses = [], []
        for gi, (win, dil) in enumerate(DIL_GROUPS):
            o, lse = dilated_causal_window_attention(
                q[:, :, gi], k[:, :, gi], v[:, :, gi], slopes[gi], win, dil)
            outs.append(o)
            lses.append(lse)
        outs = jnp.stack(outs)
        alpha = jax.nn.softmax(jnp.stack(lses), axis=0)
        y_attn = jnp.sum(alpha[..., None] * outs, axis=0).reshape(B, S, ATTN_OUT_W).astype(x.dtype)

        y_sgu = spatial_gating(uv, w_sgu_spatial[l], b_sgu_spatial[l], g_sgu[l])

        gates = jax.nn.sigmoid((gl + b_gate[l]).astype(jnp.float32)).astype(x.dtype)
        gates = gates.reshape(B, S, N_BRANCH, D)
        merged = (gates[:, :, 0] * (y_attn @ w_branch_attn[l])
                  + gates[:, :, 1] * (y_sgu @ w_branch_sgu[l]))
        h = h + merged @ w_out[l]

        c = rmsnorm(h, g_cross[l])
        m = rmsnorm(mem, g_mem[l])
        h = h + memory_cross_attention(c, m, w_q_cross[l], w_kv_cross[l], w_o_cross[l])

        f = rmsnorm(h, g_ffn[l])
        gt, up = jnp.split(f @ w_gate_up[l], 2, axis=-1)
        h = h + (jax.nn.silu(gt) * up) @ w_down[l]
    return rmsnorm(h, g_final)
```

```python
import math
from contextlib import ExitStack

import numpy as np
import concourse.bass as bass
import concourse.mybir as mybir
from concourse.bass_utils import run_bass_kernel_spmd

F32 = mybir.dt.float32
BF16 = mybir.dt.bfloat16
AF = mybir.ActivationFunctionType
ALU = mybir.AluOpType

ENGS = ("pe", "act", "dve", "pool", "sp")

D = 1024
NT = 2048
HALO = 2048
IN_W = 5376
D_FF = 2816
EPS = 1e-6
GH = (128, 512, 2048)
DIL = (1, 4, 16)


class Buf:
    __slots__ = ("name", "last_write", "reads")

    def __init__(self, name):
        self.name = name
        self.last_write = None
        self.reads = []


class ScrBuf(Buf):
    __slots__ = ()


class DmaSem:
    __slots__ = ("sem", "count", "name")

    def __init__(self, name):
        self.name = name
        self.sem = None
        self.count = 0


class Op:
    __slots__ = ("eng", "fn", "deps", "sig", "dma", "dma_val", "waits", "idx")

    def __init__(self, eng, fn, dma=None):
        self.eng = eng
        self.fn = fn
        self.deps = []
        self.sig = None
        self.dma = dma
        self.dma_val = None
        self.waits = None


class Sched:
    def __init__(self):
        self.q = {e: [] for e in ENGS}
        self.dmasems = []
        self.dma_ops = []
        self.fence_buf = Buf("fence")

    def new_dmasem(self, name):
        d = DmaSem(name)
        self.dmasems.append(d)
        return d

    def op(self, eng, fn, reads=(), writes=(), dma=None, extra_deps=()):
        o = Op(eng, fn, dma=dma)
        deps = list(extra_deps)
        if any(isinstance(b, ScrBuf) for b in reads) or any(isinstance(b, ScrBuf) for b in writes):
            reads = list(reads) + [self.fence_buf]
        for b in reads:
            if b.last_write is not None:
                deps.append(b.last_write)
        for b in writes:
            if b.last_write is not None:
                deps.append(b.last_write)
            deps.extend(b.reads)
        o.idx = len(self.q[eng])
        best = {}
        for d in deps:
            if d is o:
                continue
            if d.dma is None and o.dma is None and d.eng == "pe" and o.eng == "pe":
                continue
            if d.dma is not None:
                key = ("dma", id(d.dma))
                if key not in best or best[key].dma_val < d.dma_val:
                    best[key] = d
            else:
                key = ("eng", d.eng)
                if key not in best or best[key].idx < d.idx:
                    best[key] = d
        o.deps = list(best.values())
        for b in reads:
            b.reads.append(o)
        for b in writes:
            b.last_write = o
            b.reads = []
        if dma is not None:
            dma.count += 16
            o.dma_val = dma.count
            self.dma_ops.append(o)
        self.q[eng].append(o)
        return o

    def barrier(self):
        lasts = []
        for e in ENGS:
            for o in reversed(self.q[e]):
                if o.dma is None:
                    lasts.append(o)
                    break
        dmas = list(self.dma_ops)
        self.dma_ops = []
        for e in ENGS:
            self.op(e, lambda h: h.nop(), extra_deps=[o for o in lasts if o.eng != e] + dmas)

    def finalize(self):
        needs = set()
        for e in ENGS:
            for o in self.q[e]:
                for d in o.deps:
                    if d.dma is None:
                        needs.add(id(d))
        for e in ENGS:
            k = 0
            for o in self.q[e]:
                if o.dma is None and id(o) in needs:
                    k += 1
                    o.sig = k
        for e in ENGS:
            seen = {}
            for o in self.q[e]:
                w = {}
                for d in o.deps:
                    if d.dma is not None:
                        key = ("dma", id(d.dma))
                        val = d.dma_val
                        ref = d.dma
                    else:
                        key = ("eng", d.eng)
                        val = d.sig
                        ref = d.eng
                    if seen.get(key, 0) >= val:
                        continue
                    if key not in w or w[key][1] < val:
                        w[key] = (ref, val)
                for key, (ref, val) in w.items():
                    seen[key] = val
                o.waits = list(w.values())

    def emit(self, esem, h, e):
        for o in self.q[e]:
            for ref, val in o.waits:
                if isinstance(ref, DmaSem):
                    h.wait_ge(ref.sem, val)
                else:
                    h.wait_ge(esem[ref], val)
            ins = o.fn(h)
            if o.dma is not None:
                ins.then_inc(o.dma.sem, 16)
            elif o.sig is not None:
                ins.then_inc(esem[e], 1)


def _alibi_slopes():
    def pow2(n):
        start = 2.0 ** (-8.0 / n)
        return [start ** (i + 1) for i in range(n)]
    s = pow2(8) + pow2(16)[0::2][:4]
    return np.array(sorted(s, reverse=True), dtype=np.float64).reshape(3, 4)


def _mask_tables(first_in_seq):
    sl = _alibi_slopes()
    k = np.arange(128)[:, None].astype(np.float64)
    q = np.arange(128)[None, :].astype(np.float64)
    out = np.zeros((128, 2, 3, 4, 2, 128), np.float32)
    for g in range(3):
        for hh in range(4):
            s = sl[g, hh] * DIL[g]
            prev = np.where(k >= q, np.exp(-s * (q + 128 - k)), 0.0)
            cur = np.where(k <= q, np.exp(-s * (q - k)), 0.0)
            out[:, 0, g, hh, 0] = prev
            out[:, 0, g, hh, 1] = cur
            out[:, 1, g, hh, 0] = 0.0 if first_in_seq else prev
            out[:, 1, g, hh, 1] = cur
    return out.reshape(128, 2 * 3 * 4 * 256)


def build_nc(debug=False):
    nc = bass.Bass("TRN2", target_bir_lowering=False)

    def din(name, shape):
        return nc.dram_tensor(name, list(shape), F32, kind="ExternalInput").ap()

    xT_d = din("xT", [D, HALO + NT])
    memT_d = din("memT", [D, 256])
    cpack_d = din("cpack", [128, 64 + 4 * 512])
    cmask_d = din("cmask", [128, 24 * 256])
    w_in_d = din("w_in", [D, IN_W])
    w_ba_d = din("w_ba", [256, D])
    w_bs_d = din("w_bs", [512, D])
    w_out_d = din("w_out", [D, D])
    w_qc_d = din("w_qc", [D, 512])
    w_kvc_d = din("w_kvc", [D, 1024])
    w_oc_d = din("w_oc", [512, D])
    w_gu_d = din("w_gu", [D, 2 * D_FF])
    w_dn_d = din("w_dn", [D_FF, D])
    outT_d = nc.dram_tensor("outT", [D, NT], F32, kind="ExternalOutput").ap()
    dbg_d = {}
    if debug:
        for nm, shp in (("d_q", [128, 6 * 2048]), ("d_y", [128, 2 * 2048]), ("d_h1", [128, 8 * 2048]),
                        ("d_h2", [128, 8 * 2048])):
            dbg_d[nm] = nc.dram_tensor(nm, shp, F32, kind="ExternalOutput").ap()

    S = Sched()
    st = ExitStack()
    with st:
        ARENA = 52992
        arena = st.enter_context(nc.sbuf_tensor("arena", [128, ARENA], F32))
        psum = [st.enter_context(nc.psum_tensor("ps%d" % i, [128, 512], F32)) for i in range(8)]
        pbuf = [Buf("ps%d" % i) for i in range(8)]
        esem = {e: st.enter_context(nc.semaphore("s_" + e)) for e in ENGS}
        pcount = [0]

        def nbank():
            i = pcount[0] % 8
            pcount[0] += 1
            assert pbuf[i].last_write is None or len(pbuf[i].reads) > 0, "PSUM bank %d re-used before being read" % i
            return psum[i][:, :], pbuf[i]

        def fv(off, n):
            assert off % 4 == 0 and off // 4 + n <= ARENA, (off, n)
            return arena[:, off // 4: off // 4 + n]

        def bv(off, n):
            assert off % 4 == 0 and n % 2 == 0 and off // 4 + n // 2 <= ARENA, (off, n)
            return arena[:, off // 4: off // 4 + n // 2].bitcast(BF16)

        KB = 1024
        o = 0
        cpack = fv(o, 64 + 4 * 512)
        cvec = fv(o, 56); o += 256
        gsgu = fv(o, 512); o += 2 * KB
        bsgu = fv(o, 512); o += 2 * KB
        bsg16 = bv(o, 512)
        wspf = fv(o, 512); o += 2 * KB
        trilf = fv(o, 512); o += 2 * KB
        ones = bv(o, 128); o += 256
        wspT = bv(o, 512); o += 1 * KB
        cmask = bv(o, 24 * 256); o += 12 * KB
        kmT = bv(o, 4 * 256).rearrange("p (h m) -> p h m", h=4); o += 2 * KB
        vm = bv(o, 2 * 512).rearrange("p (c f) -> p c f", c=2); o += 2 * KB
        ssq = fv(o, 8); o += 32
        rsv = fv(o, 8); o += 32
        assert o <= 26 * KB, o
        o = 26 * KB
        RING_SLOT = 8 * KB
        ring = [bv(o + i * RING_SLOT, 4096) for i in range(4)]
        ring_buf = [Buf("ring%d" % i) for i in range(4)]
        ring_sem = [S.new_dmasem("ring%d" % i) for i in range(4)]
        o += 4 * RING_SLOT
        YT_OFF = o
        yT = bv(o, 2 * 2048).rearrange("p (c t) -> p c t", c=2); o += 8 * KB
        P0 = o
        B_const = Buf("const")
        B_cmask = Buf("cmask")
        B_yT = Buf("yT")

        for d in S.dmasems:
            pass
        misc_sems = {}

        def dsem(name):
            if name not in misc_sems:
                misc_sems[name] = S.new_dmasem(name)
            return misc_sems[name]

        rcount = [0]

        def load_w(src, kch, ncols):
            assert kch * ncols <= 4096
            i = rcount[0] % 4
            rcount[0] += 1
            view = ring[i][:, 0:kch * ncols].rearrange("p (c f) -> p c f", c=kch)
            srcv = src.rearrange("(c p) f -> p c f", p=128)
            S.op("pool", lambda h, v=view, s=srcv: h.dma_start(out=v, in_=s), writes=[ring_buf[i]], dma=ring_sem[i])
            return view, ring_buf[i]

        def load_w_multi(pieces):
            i = rcount[0] % 4
            rcount[0] += 1
            off = 0
            views = []
            for src, kch, ncols in pieces:
                view = ring[i][:, off:off + kch * ncols].rearrange("p (c f) -> p c f", c=kch)
                off += kch * ncols
                assert off <= 4096
                srcv = src.rearrange("(c p) f -> p c f", p=128)
                S.op("pool", lambda h, v=view, s_=srcv: h.dma_start(out=v, in_=s_), writes=[ring_buf[i]], dma=ring_sem[i])
                views.append(view)
            return views, ring_buf[i]

        def dump(name, view):
            if not debug:
                return
            dt = view.dtype
            dr = nc.dram_tensor(name, list(view.shape), dt, kind="ExternalOutput").ap()
            S.barrier()
            S.op("sp", lambda h: h.dma_start(out=dr, in_=view), dma=dsem("dbg"))
            S.barrier()

        def mm(out, lhsT, rhs, start, stop, reads, wbuf):
            S.op("pe", lambda h: h.matmul(out, lhsT=lhsT, rhs=rhs, start=start, stop=stop), reads=reads, writes=[wbuf])

        def act(out, in_, func, reads, writes, **kw):
            S.op("act", lambda h: h.activation(out=out, in_=in_, func=func, **kw), reads=reads, writes=writes)

        def dve(fn, reads, writes):
            S.op("dve", fn, reads=reads, writes=writes)

        ev_rr = [0]

        def evac_copy(out, in_, reads, writes):
            ev_rr[0] += 1
            if ev_rr[0] % 2:
                S.op("act", lambda h: h.activation(out=out, in_=in_, func=AF.Copy), reads=reads, writes=writes)
            else:
                S.op("dve", lambda h: h.tensor_copy(out=out, in_=in_), reads=reads, writes=writes)

        cs = dsem("const")
        B_ones = Buf("ones")
        S.op("dve", lambda h: h.memset(ones, 1.0), writes=[B_ones])
        S.op("dve", lambda h: h.memset(ssq, 0.0), writes=[B_ones])

        def load_consts():
            S.op("sp", lambda h: h.dma_start(out=cpack, in_=cpack_d), writes=[B_const], dma=cs)
        consts_loaded = [False]
        G_MIX, G_CROSS, G_MEM, G_FFN, G_FINAL, B_GATE = 0, 8, 16, 24, 32, 40

        def rmsnorm_fm(src_all, src_fn, src_bufs, ntok, gcol, dst_fn, dst_bufs, sq, sq_buf, rs, rs_buf, nch=8, dim=D, pool_chunks=()):
            act(sq[:, :, 0:ntok], src_all, AF.Square, src_bufs, [sq_buf])
            ps, pb = nbank()
            for c in range(nch):
                mm(ps[:, 0:ntok], ones, sq[:, c, 0:ntok], c == 0, c == nch - 1, [sq_buf, B_ones], pb)
            act(rs[:, 0:ntok], ps[:, 0:ntok], AF.Ln, [pb], [rs_buf], scale=1.0 / dim, bias=EPS)
            act(rs[:, 0:ntok], rs[:, 0:ntok], AF.Exp, [rs_buf], [rs_buf], scale=-0.5)
            if pool_chunks:
                npc = len(pool_chunks)
                nd = nch - npc
                rb = rs[:, 0:ntok].unsqueeze(1)
                dst_all = dst_fn(None)
                S.op("dve", lambda h: h.tensor_tensor(out=dst_all[:, 0:nd, :], in0=src_all[:, 0:nd, :],
                                                      in1=rb.broadcast_to([128, nd, ntok]), op=ALU.mult),
                     reads=list(src_bufs) + [rs_buf], writes=dst_bufs)
                S.op("pool", lambda h: h.tensor_tensor(out=dst_all[:, nd:nch, :], in0=src_all[:, nd:nch, :],
                                                       in1=rb.broadcast_to([128, npc, ntok]), op=ALU.mult),
                     reads=list(src_bufs) + [rs_buf], writes=dst_bufs)
            else:
                for c in range(nch):
                    dve(lambda h, c=c: h.scalar_tensor_tensor(out=dst_fn(c), in0=src_fn(c), scalar=cvec[:, gcol + c:gcol + c + 1],
                                                              in1=rs[:, 0:ntok], op0=ALU.mult, op1=ALU.mult),
                        list(src_bufs) + [rs_buf, B_const], dst_bufs)

        o = P0
        QT_OFF = o
        qT = bv(o, 6 * 2048).rearrange("p (c t) -> p c t", c=6); o += 24 * KB
        kT = []
        for g in range(3):
            L = GH[g] + NT
            kT.append(bv(o, 2 * L).rearrange("p (c t) -> p c t", c=2)); o += 4 * L
        NBLK = (17, 20, 32)
        Vt = []
        for g in range(3):
            Vt.append(bv(o, NBLK[g] * 256).rearrange("p (b f) -> p b f", b=NBLK[g])); o += NBLK[g] * 512
        aTf = bv(o, 8 * 2048).rearrange("p (c t) -> p c t", c=8); ACC_OFF = o; o += 32 * KB
        XS_OFF = o
        rsb = [fv(o + i * KB, 256) for i in range(4)]; o += 4 * KB
        XS3_OFF = o; o += 8 * KB
        SQ2_OFF = o; o += 4 * KB
        assert o <= ARENA * 4, o
        XT = 256
        xs = [fv(off_, 8 * XT).rearrange("p (c t) -> p c t", c=8) for off_ in (QT_OFF, QT_OFF + 8 * KB, YT_OFF, XS3_OFF)]
        sqs = [bv(off_, 8 * XT).rearrange("p (c t) -> p c t", c=8) for off_ in (QT_OFF + 16 * KB, QT_OFF + 20 * KB, SQ2_OFF)]
        B_qT = [Buf("qT%d" % c) for c in range(6)]
        B_kT = [[Buf("kT%d_%d" % (g, c)) for c in range(2)] for g in range(3)]
        B_V = [Buf("V%d" % g) for g in range(3)]
        B_aT = [Buf("aT%d" % t) for t in range(4)]
        B_xs = [Buf("xs%d" % i) for i in range(4)]
        B_sqs = [Buf("sqs%d" % i) for i in range(3)]
        B_rs = [Buf("rs%d" % i) for i in range(4)]
        xs_sem = [dsem("xs%d" % i) for i in range(4)]
        xTv = xT_d.rearrange("(c p) t -> p c t", p=128)

        def load_w_g(src, ncols):
            wv, wb = load_w(src, 8, ncols)
            for c in range(8):
                dve(lambda h, c=c, wv=wv: h.tensor_scalar(out=wv[:, c, :], in0=wv[:, c, :], scalar1=cvec[:, G_MIX + c:G_MIX + c + 1],
                                                          scalar2=None, op0=ALU.mult), [wb, B_const], [wb])
            return wv, wb

        def kproj(sb, t, wblk):
            for fc in range(6, 12):
                g, c2 = divmod(fc - 6, 2)
                if sb == 1:
                    tok0, ntok = t * 512, 512
                else:
                    lo = max(t * 512, 2048 - GH[g])
                    if lo >= (t + 1) * 512:
                        continue
                    tok0, ntok = lo, (t + 1) * 512 - lo
                wv, wb = wblk[fc // 4]
                fo = (fc % 4) * 128
                ps, pb = nbank()
                for c in range(8):
                    mm(ps[:, 0:ntok], wv[:, c, fo:fo + 128], aTf[:, c, tok0:tok0 + ntok],
                       c == 0, c == 7, [wb, B_aT[t]], pb)
                dst0 = (tok0 - (2048 - GH[g])) if sb == 0 else GH[g] + tok0
                evac_copy(kT[g][:, c2, dst0:dst0 + ntok], ps[:, 0:ntok], [pb], [B_kT[g][c2]])

        load_consts()
        dve(lambda h: h.tensor_tensor(out=wspT, in0=wspf, in1=trilf, op=ALU.mult), [B_const], [B_const])
        dve(lambda h: h.tensor_copy(out=bsg16, in_=bsgu), [B_const], [B_const])
        xcount = [0]
        for sb in range(2):
            wblk = {}
            for b in ((0, 1, 2) if sb == 1 else (1, 2)):
                wblk[b] = load_w_g(w_in_d[:, b * 512:(b + 1) * 512], 512)
            NXT = 2048 // XT
            pend = {}

            def stA(j):
                i = xcount[0] % 4
                i3 = xcount[0] % 3
                xcount[0] += 1
                t0 = sb * 2048 + j * XT
                S.op("sp", lambda h, i=i, t0=t0: h.dma_start(out=xs[i], in_=xTv[:, :, t0:t0 + XT]),
                     writes=[B_xs[i]], dma=xs_sem[i])
                act(sqs[i3][:, :, 0:XT], xs[i], AF.Square, [B_xs[i]], [B_sqs[i3]])
                ps, pb = nbank()
                for c in range(8):
                    mm(ps[:, 0:XT], ones, sqs[i3][:, c, 0:XT], c == 0, c == 7, [B_sqs[i3], B_ones], pb)
                pend[j] = (i, ps, pb)

            def stB(j):
                i, ps, pb = pend.pop(j)
                t = (j * XT) // 512
                rs = rsb[i]
                act(rs[:, 0:XT], ps[:, 0:XT], AF.Ln, [pb], [B_rs[i]], scale=1.0 / D, bias=EPS)
                act(rs[:, 0:XT], rs[:, 0:XT], AF.Exp, [B_rs[i]], [B_rs[i]], scale=-0.5)
                rb = rs[:, 0:XT].unsqueeze(1)
                dst = aTf[:, :, j * XT:(j + 1) * XT]
                S.op("dve", lambda h: h.tensor_tensor(out=dst[:, 0:5, :], in0=xs[i][:, 0:5, :],
                                                      in1=rb.broadcast_to([128, 5, XT]), op=ALU.mult),
                     reads=[B_xs[i], B_rs[i]], writes=[B_aT[t]])
                S.op("pool", lambda h: h.tensor_tensor(out=dst[:, 5:8, :], in0=xs[i][:, 5:8, :],
                                                       in1=rb.broadcast_to([128, 3, XT]), op=ALU.mult),
                     reads=[B_xs[i], B_rs[i]], writes=[B_aT[t]])

            stA(0)
            for j in range(NXT):
                if j + 1 < NXT:
                    stA(j + 1)
                stB(j)
                if (j + 1) % (512 // XT) == 0:
                    t = (j + 1) // (512 // XT) - 1
                    if t >= 1:
                        kproj(sb, t - 1, wblk)
            kproj(sb, 3, wblk)
            if sb == 1:
                for t in range(4):
                    for fc in range(6):
                        wv, wb = wblk[fc // 4]
                        fo = (fc % 4) * 128
                        ps, pb = nbank()
                        for c in range(8):
                            mm(ps, wv[:, c, fo:fo + 128], aTf[:, c, t * 512:(t + 1) * 512], c == 0, c == 7, [wb] + B_aT, pb)
                        evac_copy(qT[:, fc, t * 512:(t + 1) * 512], ps, [pb], [B_qT[fc]])
            for g in range(3):
                wv, wb = load_w_g(w_in_d[:, 1536 + g * 256:1536 + (g + 1) * 256], 256)
                blocks = []
                if g == 0:
                    if sb == 0:
                        blocks.append((0, slice(1920, 2048)))
                    else:
                        blocks += [(1 + n, slice(n * 128, (n + 1) * 128)) for n in range(16)]
                elif g == 1:
                    if sb == 0:
                        blocks += [(r, slice(1536 + r, 2048, 4)) for r in range(4)]
                    else:
                        blocks += [(4 + n1 * 4 + r, slice(512 * n1 + r, 512 * (n1 + 1), 4)) for n1 in range(4) for r in range(4)]
                else:
                    blocks += [(sb * 16 + r, slice(r, 2048, 16)) for r in range(16)]
                for p in range(0, len(blocks), 2):
                    grp = blocks[p:p + 2]
                    ps, pb = nbank()
                    for bi, (blk, sl) in enumerate(grp):
                        for c in range(8):
                            mm(ps[:, bi * 256:(bi + 1) * 256], aTf[:, c, sl], wv[:, c, :], c == 0, c == 7, [wb] + B_aT, pb)
                    b0 = grp[0][0]
                    n = len(grp)
                    evac_copy(Vt[g][:, b0:b0 + n, :], ps[:, 0:n * 256].rearrange("p (b f) -> p b f", b=n), [pb], [B_V[g]])

        S.op("pool", lambda h: h.dma_start(out=cmask, in_=cmask_d), writes=[B_cmask], dma=dsem("cmask"))
        if debug:
            S.barrier()
            dq = dsem("dbg")
            stg = fv(XS_OFF, 2048)
            B_stg = Buf("stg")
            for c in range(6):
                dve(lambda h, c=c: h.tensor_copy(out=stg, in_=qT[:, c, :]), [B_qT[c]], [B_stg])
                S.op("sp", lambda h, c=c: h.dma_start(out=dbg_d["d_q"][:, c * 2048:(c + 1) * 2048], in_=stg),
                     reads=[B_stg], dma=dq)
        dump("d_cmask", cmask)
        dump("d_kT0", kT[0].rearrange("p c t -> p (c t)"))
        dump("d_V0", Vt[0].rearrange("p b f -> p (b f)"))
        S.barrier()

        acc = fv(ACC_OFF, 2 * 2 * 2048).rearrange("p (c n t) -> p c n t", c=2, n=2)
        o = QT_OFF
        o = XS_OFF
        NEB = 3
        ET = [[bv(o + (2 * i + hh) * KB, 512) for hh in range(2)] for i in range(NEB)]; o += 2 * NEB * KB
        EM = [[bv(o + (2 * i + hh) * KB, 512) for hh in range(2)] for i in range(NEB)]; o += 2 * NEB * KB
        assert o <= ARENA * 4, o
        B_ET = [[ScrBuf("ET%d%d" % (i, hh)) for hh in range(2)] for i in range(NEB)]
        B_EM = [[ScrBuf("EM%d%d" % (i, hh)) for hh in range(2)] for i in range(NEB)]
        B_acc = [ScrBuf("acc%d" % c) for c in range(2)]
        cmv = cmask.rearrange("p (v g h x) -> p v g h x", v=2, g=3, h=4)

        def qsl(g, n):
            if g == 0:
                return slice(n * 128, (n + 1) * 128)
            if g == 1:
                n1, r = divmod(n, 4)
                return slice(512 * n1 + r, 512 * (n1 + 1), 4)
            return slice(n, 2048, 16)

        def ksl(g, n, kb):
            s_ = qsl(g, n)
            off = GH[g] if kb == 1 else GH[g] - 128 * DIL[g]
            return slice(s_.start + off, s_.stop + off, s_.step)

        def vblk(g, n, kb):
            if g == 0:
                return n + kb
            if g == 1:
                n1, r = divmod(n, 4)
                return (n1 + kb) * 4 + r
            return kb * 16 + n

        def is_halo(g, n):
            return (g == 0 and n == 0) or (g == 1 and n < 4) or g == 2

        iters = [(g, hp, npair) for g in range(3) for hp in range(2) for npair in range(8)]
        NI = len(iters)
        sps_of = {}

        def st_S(i):
            g, hp, npair = iters[i]
            ch = 2 * g + hp
            nn = (2 * npair, 2 * npair + 1)
            sps = []
            for hh in range(2):
                bk = (2 * i + hh) % 6
                ps, pb = psum[bk][:, :], pbuf[bk]
                sps.append((ps, pb))
                pr = slice(hh * 64, (hh + 1) * 64)
                for qi, n in enumerate(nn):
                    for kb in range(2):
                        col = (qi * 2 + kb) * 128
                        mm(ps[:, col:col + 128], kT[g][pr, hp, ksl(g, n, kb)], qT[pr, ch, qsl(g, n)],
                           True, True, [B_kT[g][hp], B_qT[ch]], pb)
            sps_of[i] = sps

        def st_E(i):
            g, hp, npair = iters[i]
            nn = (2 * npair, 2 * npair + 1)
            bi = i % NEB
            for hh in range(2):
                ps, pb = sps_of[i][hh]
                act(ET[bi][hh], ps, AF.Exp, [pb], [B_ET[bi][hh]], scale=0.125)
                vs = [1 if is_halo(g, n) else 0 for n in nn]
                eng = "dve" if hh == 0 else "pool"
                if vs[0] == vs[1]:
                    S.op(eng, lambda h, bi=bi, hh=hh, v=vs[0], g=g, hp=hp: h.tensor_tensor(
                        out=EM[bi][hh].rearrange("p (q x) -> p q x", q=2), in0=ET[bi][hh].rearrange("p (q x) -> p q x", q=2),
                        in1=cmv[:, v, g, 2 * hp + hh, :].unsqueeze(1).broadcast_to([128, 2, 256]), op=ALU.mult),
                        reads=[B_ET[bi][hh], B_cmask], writes=[B_EM[bi][hh]])
                else:
                    for qi, n in enumerate(nn):
                        S.op(eng, lambda h, bi=bi, hh=hh, qi=qi, v=vs[qi], g=g, hp=hp: h.tensor_tensor(
                            out=EM[bi][hh][:, qi * 256:(qi + 1) * 256], in0=ET[bi][hh][:, qi * 256:(qi + 1) * 256],
                            in1=cmv[:, v, g, 2 * hp + hh, :], op=ALU.mult),
                            reads=[B_ET[bi][hh], B_cmask], writes=[B_EM[bi][hh]])

        def st_P(i):
            g, hp, npair = iters[i]
            nn = (2 * npair, 2 * npair + 1)
            bi = i % NEB
            bk = 6 + (i % 2)
            ps, pb = psum[bk][:, :], pbuf[bk]
            for hh in range(2):
                pr = slice(hh * 64, (hh + 1) * 64)
                hcol = slice((2 * hp + hh) * 64, (2 * hp + hh + 1) * 64)
                for qi, n in enumerate(nn):
                    for kb in range(2):
                        e = EM[bi][hh][:, (qi * 2 + kb) * 128:(qi * 2 + kb + 1) * 128]
                        mm(ps[pr, qi * 128:(qi + 1) * 128], Vt[g][:, vblk(g, n, kb), hcol], e,
                           kb == 0, kb == 1, [B_V[g], B_EM[bi][hh]], pb)
                    for kb in range(2):
                        e = EM[bi][hh][:, (qi * 2 + kb) * 128:(qi * 2 + kb + 1) * 128]
                        mm(ps[pr, 256 + qi * 128:256 + (qi + 1) * 128], ones[:, 0:64], e,
                           kb == 0, kb == 1, [B_ones, B_EM[bi][hh]], pb)
            n0 = nn[0]
            if g == 0:
                dst = acc[:, hp, :, n0 * 128:(n0 + 2) * 128].rearrange("p n (q i) -> p n q i", q=2)
            elif g == 1:
                n1, r = divmod(n0, 4)
                dst = acc[:, hp, :, 512 * n1:512 * (n1 + 1)].rearrange("p n (i r) -> p n r i", r=4)[:, :, r:r + 2, :]
            else:
                dst = acc[:, hp, :, :].rearrange("p n (i r) -> p n r i", r=16)[:, :, n0:n0 + 2, :]
            src = ps.rearrange("p (n q i) -> p n q i", n=2, q=2)
            if g == 0:
                act(dst, src, AF.Copy, [pb], [B_acc[hp]])
            else:
                dve(lambda h, dst=dst, src=src: h.tensor_tensor(out=dst, in0=src, in1=dst, op=ALU.add),
                    [pb, B_acc[hp]], [B_acc[hp]])

        st_S(0)
        st_S(1)
        st_E(0)
        for i in range(NI):
            if i + 2 < NI:
                st_S(i + 2)
            if i + 1 < NI:
                st_E(i + 1)
            st_P(i)
        dump("d_acc", acc.rearrange("p c n t -> p (c n t)"))
        for hp in range(2):
            act(acc[:, hp, 1, :], acc[:, hp, 1, :], AF.Ln, [B_acc[hp]], [B_acc[hp]])
            act(acc[:, hp, 1, :], acc[:, hp, 1, :], AF.Exp, [B_acc[hp]], [B_acc[hp]], scale=-1.0)
            dve(lambda h, hp=hp: h.tensor_tensor(out=yT[:, hp, :], in0=acc[:, hp, 0, :], in1=acc[:, hp, 1, :], op=ALU.mult),
                [B_acc[hp]], [B_yT])
        if debug:
            S.barrier()
            stg = fv(XS_OFF + 8 * KB, 2048)
            for c in range(2):
                dve(lambda h, c=c: h.tensor_copy(out=stg, in_=yT[:, c, :]), [B_yT], [B_stg])
                S.op("sp", lambda h, c=c: h.dma_start(out=dbg_d["d_y"][:, c * 2048:(c + 1) * 2048], in_=stg),
                     reads=[B_stg], dma=dq)
            S.barrier()

        o = P0
        hT = fv(o, 8 * 2048).rearrange("p (c t) -> p c t", c=8); o += 64 * KB
        HT = 1024
        nT = bv(o, 8 * HT).rearrange("p (c t) -> p c t", c=8); o += 16 * KB
        sq3 = bv(o, 8 * 512).rearrange("p (c t) -> p c t", c=8); o += 8 * KB
        rs3 = [fv(o + i * 2 * KB, 512) for i in range(2)]; o += 4 * KB
        SCR = o
        B_hT = [[Buf("hT%d_%d" % (c, t)) for t in range(4)] for c in range(8)]
        B_nT = [Buf("nT%d" % t) for t in range(2)]
        B_sq3 = Buf("sq3")
        B_rs3 = [Buf("rs3_%d" % i) for i in range(2)]
        hs = dsem("hT")
        p12_bufs = B_qT + [b_ for g_ in range(3) for b_ in B_kT[g_]] + B_V + B_aT
        hprev = None
        for t in range(4):
            hprev = S.op("sp", lambda h, t=t: h.dma_start(out=hT[:, :, t * 512:(t + 1) * 512],
                                                           in_=xTv[:, :, HALO + t * 512:HALO + (t + 1) * 512]),
                         writes=[B_hT[c][t] for c in range(8)] + p12_bufs, dma=hs,
                         extra_deps=([hprev] if (hprev is not None and t >= 1) else []))
        S.op("sp", lambda h: h.nop(), writes=[S.fence_buf])

        B_kv = Buf("kvmem")

        def kv_mem(o):
            memf = fv(o, 8 * 256).rearrange("p (c t) -> p c t", c=8); o += 8 * KB
            mnT = bv(o, 8 * 256).rearrange("p (c t) -> p c t", c=8); o += 4 * KB
            assert o <= ARENA * 4, o
            B_memf = ScrBuf("memf"); B_mnT = ScrBuf("mnT")
            S.op("sp", lambda h: h.dma_start(out=memf, in_=memT_d.rearrange("(c p) t -> p c t", p=128)),
                 writes=[B_memf], dma=dsem("memf"))
            rmsnorm_fm(memf, lambda c: memf[:, c, :], [B_memf], 256, G_MEM, lambda c: mnT[:, c, :], [B_mnT],
                       sq3, B_sq3, rs3[0], B_rs3[0])
            wv, wb = load_w(w_kvc_d[:, 0:512], 8, 512)
            for hd in range(4):
                ps, pb = nbank()
                for c in range(8):
                    mm(ps[:, 0:256], wv[:, c, hd * 128:(hd + 1) * 128], mnT[:, c, :], c == 0, c == 7, [wb, B_mnT], pb)
                evac_copy(kmT[:, hd, :], ps[:, 0:256], [pb], [B_kv])
            wv, wb = load_w(w_kvc_d[:, 512:1024], 8, 512)
            for mc in range(2):
                ps, pb = nbank()
                for c in range(8):
                    mm(ps, mnT[:, c, mc * 128:(mc + 1) * 128], wv[:, c, :], c == 0, c == 7, [wb, B_mnT], pb)
                evac_copy(vm[:, mc, :], ps, [pb], [B_kv])

        def norm_tile(H, tt, gcol):
            t = 2 * H + tt
            rmsnorm_fm(hT[:, :, t * 512:(t + 1) * 512], lambda c, t=t: hT[:, c, t * 512:(t + 1) * 512], [B_hT[c][t] for c in range(8)], 512, gcol,
                       lambda c, tt=tt: nT[:, c, tt * 512:(tt + 1) * 512], [B_nT[tt]],
                       sq3, B_sq3, rs3[tt], B_rs3[tt])

        def norm_half(H, gcol):
            for tt in range(2):
                norm_tile(H, tt, gcol)

        def resid_add(ps, pb, m, t):
            dve(lambda h: h.tensor_tensor(out=hT[:, m, t * 512:(t + 1) * 512], in0=ps, in1=hT[:, m, t * 512:(t + 1) * 512],
                                          op=ALU.add), [pb, B_hT[m][t]], [B_hT[m][t]])

        import os
        K_FENCE = os.environ.get("K_FENCE", "1") == "1"
        K_HOIST = os.environ.get("K_HOIST", "1") == "1"
        K_PIPE = os.environ.get("K_PIPE", "1") == "1"

        def fence():
            if K_FENCE:
                S.op("sp", lambda h: h.nop(), writes=[S.fence_buf])
            else:
                S.barrier()

        for H in range(2):
            tok0 = H * HT
            o = SCR
            uT = bv(o, 4 * HT).rearrange("p (c t) -> p c t", c=4); o += 8 * KB
            ysT = bv(o, 4 * HT).rearrange("p (c t) -> p c t", c=4); o += 8 * KB
            vg = fv(o, 8 * 512).rearrange("p (b f) -> p b f", b=8)
            mgT = bv(o, 8 * HT).rearrange("p (c t) -> p c t", c=8); o += 16 * KB
            NVN = 4
            vn = [bv(o + i * KB, 512) for i in range(NVN)]; o += NVN * KB
            mxb = [fv(o + i * 2 * KB, 512) for i in range(2)]; o += 4 * KB
            gt = [bv(o + i * KB, 512) for i in range(4)]; o += 4 * KB
            t1b = [fv(o + i * 2 * KB, 512) for i in range(2)]; o += 4 * KB
            assert o <= ARENA * 4, o
            B_uT = ScrBuf("uT"); B_ysT = ScrBuf("ysT"); B_vg = ScrBuf("vg"); B_vn = [ScrBuf("vn%d" % i) for i in range(NVN)]
            B_mx = [ScrBuf("mx0"), ScrBuf("mx1")]; B_mg = [ScrBuf("mg%d" % t) for t in range(2)]
            B_gt = [ScrBuf("gt%d" % i) for i in range(4)]; B_t1 = [ScrBuf("t1_0"), ScrBuf("t1_1")]
            B_ssq = Buf("ssq")
            if H == 1 and not K_HOIST:
                norm_half(1, G_MIX)
            dve(lambda h: h.memset(ssq, 0.0), [], [B_ssq])
            wvu, wbu = load_w(w_in_d[:, 2304:2816], 8, 512)
            wvv, wbv = load_w(w_in_d[:, 2816:3328], 8, 512)
            for tt in range(2):
                if H == 0:
                    norm_tile(0, tt, G_MIX)
                for blk in range(4 * tt, 4 * tt + 4):
                    bo = (blk * 128) % 512
                    ps, pb = nbank()
                    for c in range(8):
                        mm(ps, nT[:, c, tt * 512 + bo:tt * 512 + bo + 128], wvv[:, c, :], c == 0, c == 7, [wbv, B_nT[tt]], pb)
                    act(vg[:, blk, :], ps, AF.Gelu, [pb], [B_vg])
                    act(t1b[blk % 2], vg[:, blk, :], AF.Square, [B_vg], [B_t1[blk % 2], B_ssq], accum_out=ssq[:, blk:blk + 1])
                for fc in range(4):
                    ps, pb = nbank()
                    for c in range(8):
                        mm(ps, wvu[:, c, fc * 128:(fc + 1) * 128], nT[:, c, tt * 512:(tt + 1) * 512], c == 0, c == 7,
                           [wbu, B_nT[tt]], pb)
                    act(uT[:, fc, tt * 512:(tt + 1) * 512], ps, AF.Gelu, [pb], [B_uT])
            act(rsv, ssq, AF.Ln, [B_ssq], [B_ssq], scale=1.0 / 512, bias=EPS)
            act(rsv, rsv, AF.Exp, [B_ssq], [B_ssq], scale=-0.5)
            for blk in range(8):
                i = blk % NVN
                dve(lambda h, blk=blk, i=i: h.scalar_tensor_tensor(out=vn[i], in0=vg[:, blk, :], scalar=rsv[:, blk:blk + 1],
                                                                    in1=gsgu, op0=ALU.mult, op1=ALU.mult),
                    [B_vg, B_ssq, B_const], [B_vn[i]])
                ps, pb = nbank()
                mm(ps, ones[0:1, 0:128], bsg16[0:1, :], True, False, [B_ones, B_const], pb)
                for gi in range(4):
                    mm(ps[:, gi * 128:(gi + 1) * 128], vn[i][:, gi * 128:(gi + 1) * 128], wspT[:, gi * 128:(gi + 1) * 128],
                       False, gi == 3, [B_vn[i], B_const], pb)
                dve(lambda h, ps=ps, blk=blk: h.tensor_tensor(
                    out=ysT[:, :, blk * 128:(blk + 1) * 128], in0=ps.rearrange("p (g t) -> p g t", g=4),
                    in1=uT[:, :, blk * 128:(blk + 1) * 128], op=ALU.mult), [pb, B_uT], [B_ysT])
            for m in range(8):
                fsl = slice(m * 128, (m + 1) * 128)
                (wg0, wg1, wba, wbs), wsb = load_w_multi([
                    (w_in_d[:, 3328 + m * 128:3328 + (m + 1) * 128], 8, 128),
                    (w_in_d[:, 4352 + m * 128:4352 + (m + 1) * 128], 8, 128),
                    (w_ba_d[:, fsl], 2, 128),
                    (w_bs_d[:, fsl], 4, 128)])
                for tt in range(2):
                    ts = slice(tt * 512, (tt + 1) * 512)
                    tsg = slice(tok0 + tt * 512, tok0 + (tt + 1) * 512)
                    gi0 = (2 * tt) % 4
                    for k, (wv_, boff) in enumerate(((wg0, 0), (wg1, 8))):
                        ps, pb = nbank()
                        for c in range(8):
                            mm(ps, wv_[:, c, :], nT[:, c, ts], c == 0, c == 7, [wsb, B_nT[tt]], pb)
                        act(gt[gi0 + k], ps, AF.Sigmoid, [pb, B_const], [B_gt[gi0 + k]],
                            bias=cvec[:, B_GATE + boff + m:B_GATE + boff + m + 1])
                    ps, pb = nbank()
                    for c in range(2):
                        mm(ps, wba[:, c, :], yT[:, c, tsg], c == 0, c == 1, [wsb, B_yT], pb)
                    dve(lambda h, ps=ps, i=tt, gi0=gi0: h.tensor_tensor(out=t1b[i], in0=ps, in1=gt[gi0], op=ALU.mult),
                        [pb, B_gt[gi0]], [B_t1[tt]])
                    ps, pb = nbank()
                    for c in range(4):
                        mm(ps, wbs[:, c, :], ysT[:, c, ts], c == 0, c == 3, [wsb, B_ysT], pb)
                    dve(lambda h, ps=ps, i=tt, gi0=gi0: h.tensor_tensor(out=mxb[i], in0=ps, in1=gt[gi0 + 1], op=ALU.mult),
                        [pb, B_gt[gi0 + 1]], [B_mx[tt]])
                    dve(lambda h, i=tt, m=m, ts=ts: h.tensor_tensor(out=mgT[:, m, ts], in0=t1b[i], in1=mxb[i], op=ALU.add),
                        [B_t1[tt], B_mx[tt]], [B_mg[tt], B_vg])
            wo = [load_w(w_out_d[:, mg * 512:(mg + 1) * 512], 8, 512) for mg in range(2)]
            for tt in range(2):
                for m in range(8):
                    wv, wb = wo[m // 4]
                    mc = m % 4
                    ps, pb = nbank()
                    for c in range(8):
                        mm(ps, wv[:, c, mc * 128:(mc + 1) * 128], mgT[:, c, tt * 512:(tt + 1) * 512], c == 0, c == 7,
                           [wb, B_mg[tt]], pb)
                    resid_add(ps, pb, m, 2 * H + tt)
            fence()
            o = SCR
            qcT = bv(o, 4 * HT).rearrange("p (c t) -> p c t", c=4); o += 8 * KB
            ocT = bv(o, 4 * HT).rearrange("p (c t) -> p c t", c=4); o += 8 * KB
            NEC = 3
            ec = [bv(o + i * 2 * KB, 1024).rearrange("p (c t) -> p c t", c=2) for i in range(NEC)]; o += NEC * 2 * KB
            rd = [fv(o + i * 2 * KB, 512) for i in range(2)]; o += 4 * KB
            B_qc = [ScrBuf("qcT0"), ScrBuf("qcT1")]; B_oc = [ScrBuf("ocT0"), ScrBuf("ocT1")]
            B_ec = [ScrBuf("ec%d" % i) for i in range(NEC)]; B_rd = [ScrBuf("rd0"), ScrBuf("rd1")]
            wv, wb = load_w(w_qc_d, 8, 512)
            for tt in range(2):
                norm_tile(H, tt, G_CROSS)
                for hd in range(4):
                    ps, pb = nbank()
                    for c in range(8):
                        mm(ps, wv[:, c, hd * 128:(hd + 1) * 128], nT[:, c, tt * 512:(tt + 1) * 512], c == 0, c == 7,
                           [wb, B_nT[tt]], pb)
                    evac_copy(qcT[:, hd, tt * 512:(tt + 1) * 512], ps, [pb], [B_qc[tt]])
                if H == 0 and tt == 0:
                    kv_mem(o)
            woc = [load_w(w_oc_d[:, mg * 512:(mg + 1) * 512], 4, 512) for mg in range(2)]
            cits = [(tt, hd) for tt in range(2) for hd in range(4)]
            csp = {}

            def c_S(i):
                tt, hd = cits[i]
                ts = slice(tt * 512, (tt + 1) * 512)
                lst = []
                for mc in range(2):
                    ps, pb = nbank()
                    mm(ps, kmT[:, hd, mc * 128:(mc + 1) * 128], qcT[:, hd, ts], True, True, [B_kv, B_qc[tt]], pb)
                    lst.append((ps, pb))
                csp[i] = lst

            def c_E(i):
                k = i % NEC
                for mc in range(2):
                    ps, pb = csp[i][mc]
                    act(ec[k][:, mc, :], ps, AF.Exp, [pb], [B_ec[k]], scale=1.0 / math.sqrt(128.0))

            def c_P(i):
                tt, hd = cits[i]
                ts = slice(tt * 512, (tt + 1) * 512)
                k = i % NEC
                j = i % 2
                psn, pbn = nbank()
                for mc in range(2):
                    mm(psn, vm[:, mc, hd * 128:(hd + 1) * 128], ec[k][:, mc, :], mc == 0, mc == 1, [B_kv, B_ec[k]], pbn)
                psd, pbd = nbank()
                for mc in range(2):
                    mm(psd, ones, ec[k][:, mc, :], mc == 0, mc == 1, [B_ones, B_ec[k]], pbd)
                act(rd[j], psd, AF.Ln, [pbd], [B_rd[j]])
                act(rd[j], rd[j], AF.Exp, [B_rd[j]], [B_rd[j]], scale=-1.0)
                dve(lambda h: h.tensor_tensor(out=ocT[:, hd, ts], in0=psn, in1=rd[j], op=ALU.mult),
                    [pbn, B_rd[j]], [B_oc[tt]])

            def c_O(tt):
                for m in range(8):
                    wv, wb = woc[m // 4]
                    mc = m % 4
                    ps, pb = nbank()
                    for c in range(4):
                        mm(ps, wv[:, c, mc * 128:(mc + 1) * 128], ocT[:, c, tt * 512:(tt + 1) * 512], c == 0, c == 3,
                           [wb, B_oc[tt]], pb)
                    resid_add(ps, pb, m, 2 * H + tt)

            if K_PIPE:
                for tt in range(2):
                    b0 = 4 * tt
                    c_S(b0); c_S(b0 + 1); c_E(b0)
                    for i in range(b0, b0 + 4):
                        if i + 2 < b0 + 4:
                            c_S(i + 2)
                        if i + 1 < b0 + 4:
                            c_E(i + 1)
                        c_P(i)
                    c_O(tt)
            else:
                for i in range(8):
                    c_S(i); c_E(i); c_P(i)
                    if i == 3:
                        c_O(0)
                c_O(1)
            fence()
            o = SCR
            hid = bv(o, 22 * HT).rearrange("p (c t) -> p c t", c=22); o += 44 * KB
            sg = [bv(o + i * KB, 512) for i in range(2)]; o += 2 * KB
            assert o <= ARENA * 4, o
            B_hid = [ScrBuf("hid%d" % t) for t in range(2)]
            B_sg = [ScrBuf("sg0"), ScrBuf("sg1")]
            norm_tile(H, 0, G_FFN)
            its = 0
            for jb in range(0, 22, 4):
                nj = min(4, 22 - jb)
                wg, wgb = load_w(w_gu_d[:, jb * 128:(jb + nj) * 128], 8, nj * 128)
                wu, wub = load_w(w_gu_d[:, D_FF + jb * 128:D_FF + (jb + nj) * 128], 8, nj * 128)
                order = [(jj, tt) for jj in range(nj) for tt in range(2)] if jb > 0 else \
                        [(jj, tt) for tt in range(2) for jj in range(nj)]
                for jj, tt in order:
                    if jb == 0 and tt == 1 and jj == 0:
                        norm_tile(H, 1, G_FFN)
                    if True:
                        j = jb + jj
                        fs = slice(jj * 128, (jj + 1) * 128)
                        ts = slice(tt * 512, (tt + 1) * 512)
                        i = its % 2
                        its += 1
                        psg, pbg = nbank()
                        for c in range(8):
                            mm(psg, wg[:, c, fs], nT[:, c, ts], c == 0, c == 7, [wgb, B_nT[tt]], pbg)
                        psu, pbu = nbank()
                        for c in range(8):
                            mm(psu, wu[:, c, fs], nT[:, c, ts], c == 0, c == 7, [wub, B_nT[tt]], pbu)
                        act(sg[i], psg, AF.Silu, [pbg], [B_sg[i]])
                        dve(lambda h, i=i, psu=psu, j=j, ts=ts: h.tensor_tensor(out=hid[:, j, ts], in0=psu, in1=sg[i], op=ALU.mult),
                            [pbu, B_sg[i]], [B_hid[tt]])
            def down_group(m, wv, wb, tt):
                ps, pb = nbank()
                for j in range(22):
                    mm(ps, wv[:, j, :], hid[:, j, tt * 512:(tt + 1) * 512], j == 0, j == 21, [wb, B_hid[tt]], pb)
                resid_add(ps, pb, m, 2 * H + tt)

            for m in range(4):
                wv, wb = load_w(w_dn_d[:, m * 128:(m + 1) * 128], 22, 128)
                for tt in range(2):
                    down_group(m, wv, wb, tt)
                if H == 0 and K_HOIST and m < 2:
                    norm_tile(1, m, G_MIX)
            wd = [load_w(w_dn_d[:, m * 128:(m + 1) * 128], 22, 128) for m in range(4, 8)]
            for tt in range(2):
                for mi, m in enumerate(range(4, 8)):
                    down_group(m, wd[mi][0], wd[mi][1], tt)
            fence()
            o = SCR
            ost = [fv(o + i * 16 * KB, 8 * 512).rearrange("p (c t) -> p c t", c=8) for i in range(2)]; o += 32 * KB
            B_ost = [ScrBuf("ost0"), ScrBuf("ost1")]
            osem = [dsem("ost0"), dsem("ost1")]
            outv = outT_d.rearrange("(c p) t -> p c t", p=128)
            for tt in range(2):
                t = 2 * H + tt
                rmsnorm_fm(hT[:, :, t * 512:(t + 1) * 512], lambda c, t=t: hT[:, c, t * 512:(t + 1) * 512], [B_hT[c][t] for c in range(8)], 512, G_FINAL,
                           lambda c, tt=tt: ost[tt][:, c, :], [B_ost[tt]],
                           sq3 if (tt == 0 or H == 0) else nT[:, :, 0:512], B_sq3 if (tt == 0 or H == 0) else B_nT[0],
                           rs3[tt], B_rs3[tt])
                S.op("sp", lambda h, tt=tt, t=t: h.dma_start(out=outv[:, :, t * 512:(t + 1) * 512], in_=ost[tt]),
                     reads=[B_ost[tt]], dma=osem[tt])
            fence()
        S.barrier()

        for d in S.dmasems:
            d.sem = st.enter_context(nc.semaphore("d_" + d.name))
        S.finalize()
        with nc.Block() as block:
            @block.tensor
            def _(h):
                S.emit(esem, h, "pe")

            @block.scalar
            def _(h):
                S.emit(esem, h, "act")

            @block.vector
            def _(h):
                S.emit(esem, h, "dve")

            @block.gpsimd
            def _(h):
                S.emit(esem, h, "pool")

            @block.sync
            def _(h):
                S.emit(esem, h, "sp")
    return nc


def make_in_maps(inputs):
    f = lambda a: np.ascontiguousarray(np.asarray(a, dtype=np.float32))
    x = f(inputs["x"]); mem = f(inputs["mem"])
    vec8 = lambda v: np.asarray(v, np.float32).reshape(8, 128).T
    cvec = np.concatenate([vec8(inputs["g_mix"][0]), vec8(inputs["g_cross"][0]), vec8(inputs["g_mem"][0]),
                           vec8(inputs["g_ffn"][0]), vec8(inputs["g_final"]),
                           np.asarray(inputs["b_gate"][0], np.float32).reshape(16, 128).T], axis=1)
    gsgu_b = np.broadcast_to(np.asarray(inputs["g_sgu"][0], np.float32)[None, :], (128, 512))
    bsgu_b = np.broadcast_to(np.asarray(inputs["b_sgu_spatial"][0], np.float32).reshape(1, 512), (128, 512))
    wsp = np.asarray(inputs["w_sgu_spatial"][0], np.float32)
    wspT = np.transpose(wsp, (2, 0, 1)).reshape(128, 512)
    s_i = np.arange(128)[:, None]; t_i = np.arange(128)[None, :]
    trilm = np.tile((s_i <= t_i).astype(np.float32), (1, 4))
    shared = {
        "cpack": f(np.concatenate([cvec, np.zeros((128, 8), np.float32), gsgu_b, bsgu_b, wspT, trilm], axis=1)),
        "w_in": f(inputs["w_in"][0]), "w_ba": f(inputs["w_branch_attn"][0]), "w_bs": f(inputs["w_branch_sgu"][0]),
        "w_out": f(inputs["w_out"][0]), "w_qc": f(inputs["w_q_cross"][0]), "w_kvc": f(inputs["w_kv_cross"][0]),
        "w_oc": f(inputs["w_o_cross"][0]), "w_gu": f(inputs["w_gate_up"][0]), "w_dn": f(inputs["w_down"][0]),
    }
    masks = [f(_mask_tables(True)), f(_mask_tables(False))]
    maps = []
    for core in range(8):
        b, part = divmod(core, 4)
        t0 = part * NT
        xT = np.zeros((D, HALO + NT), np.float32)
        if part > 0:
            xT[:, 0:HALO] = x[b, t0 - HALO:t0, :].T
        xT[:, HALO:] = x[b, t0:t0 + NT, :].T
        m = dict(shared)
        m["xT"] = xT
        m["memT"] = f(mem[b].T)
        m["cmask"] = masks[0] if part == 0 else masks[1]
        maps.append(m)
    return maps


def kernel(**inputs):
    nc = build_nc()
    maps = make_in_maps(inputs)
    res = run_bass_kernel_spmd(nc, maps, core_ids=list(range(8)))
    out = np.zeros((2, 8192, D), np.float32)
    for core in range(8):
        b, part = divmod(core, 4)
        out[b, part * NT:(part + 1) * NT, :] = res.results[core]["outT"].T
    return out
```

```python
import math
from contextlib import ExitStack

import numpy as np
import concourse.bass as bass
import concourse.mybir as mybir
from concourse.bass_utils import run_bass_kernel_spmd

F32 = mybir.dt.float32
BF16 = mybir.dt.bfloat16
AF = mybir.ActivationFunctionType
ALU = mybir.AluOpType

ENGS = ("pe", "act", "dve", "pool", "sp")

D = 1024
NT = 2048
HALO = 2048
IN_W = 5376
D_FF = 2816
EPS = 1e-6
GH = (128, 512, 2048)
DIL = (1, 4, 16)


class Buf:
    __slots__ = ("name", "last_write", "reads")

    def __init__(self, name):
        self.name = name
        self.last_write = None
        self.reads = []


class ScrBuf(Buf):
    __slots__ = ()


class DmaSem:
    __slots__ = ("sem", "count", "name")

    def __init__(self, name):
        self.name = name
        self.sem = None
        self.count = 0


class Op:
    __slots__ = ("eng", "fn", "deps", "sig", "dma", "dma_val", "waits", "idx")

    def __init__(self, eng, fn, dma=None):
        self.eng = eng
        self.fn = fn
        self.deps = []
        self.sig = None
        self.dma = dma
        self.dma_val = None
        self.waits = None


class Sched:
    def __init__(self):
        self.q = {e: [] for e in ENGS}
        self.dmasems = []
        self.dma_ops = []
        self.fence_buf = Buf("fence")

    def new_dmasem(self, name):
        d = DmaSem(name)
        self.dmasems.append(d)
        return d

    def op(self, eng, fn, reads=(), writes=(), dma=None, extra_deps=()):
        o = Op(eng, fn, dma=dma)
        deps = list(extra_deps)
        if any(isinstance(b, ScrBuf) for b in reads) or any(isinstance(b, ScrBuf) for b in writes):
            reads = list(reads) + [self.fence_buf]
        for b in reads:
            if b.last_write is not None:
                deps.append(b.last_write)
        for b in writes:
            if b.last_write is not None:
                deps.append(b.last_write)
            deps.extend(b.reads)
        o.idx = len(self.q[eng])
        best = {}
        for d in deps:
            if d is o:
                continue
            if d.dma is None and o.dma is None and d.eng == "pe" and o.eng == "pe":
                continue
            if d.dma is not None:
                key = ("dma", id(d.dma))
                if key not in best or best[key].dma_val < d.dma_val:
                    best[key] = d
            else:
                key = ("eng", d.eng)
                if key not in best or best[key].idx < d.idx:
                    best[key] = d
        o.deps = list(best.values())
        for b in reads:
            b.reads.append(o)
        for b in writes:
            b.last_write = o
            b.reads = []
        if dma is not None:
            dma.count += 16
            o.dma_val = dma.count
            self.dma_ops.append(o)
        self.q[eng].append(o)
        return o

    def barrier(self):
        lasts = []
        for e in ENGS:
            for o in reversed(self.q[e]):
                if o.dma is None:
                    lasts.append(o)
                    break
        dmas = list(self.dma_ops)
        self.dma_ops = []
        for e in ENGS:
            self.op(e, lambda h: h.nop(), extra_deps=[o for o in lasts if o.eng != e] + dmas)

    def finalize(self):
        needs = set()
        for e in ENGS:
            for o in self.q[e]:
                for d in o.deps:
                    if d.dma is None:
                        needs.add(id(d))
        for e in ENGS:
            k = 0
            for o in self.q[e]:
                if o.dma is None and id(o) in needs:
                    k += 1
                    o.sig = k
        for e in ENGS:
            seen = {}
            for o in self.q[e]:
                w = {}
                for d in o.deps:
                    if d.dma is not None:
                        key = ("dma", id(d.dma))
                        val = d.dma_val
                        ref = d.dma
                    else:
                        key = ("eng", d.eng)
                        val = d.sig
                        ref = d.eng
                    if seen.get(key, 0) >= val:
                        continue
                    if key not in w or w[key][1] < val:
                        w[key] = (ref, val)
                for key, (ref, val) in w.items():
                    seen[key] = val
                o.waits = list(w.values())

    def emit(self, esem, h, e):
        for o in self.q[e]:
            for ref, val in o.waits:
                if isinstance(ref, DmaSem):
                    h.wait_ge(ref.sem, val)
                else:
                    h.wait_ge(esem[ref], val)
            ins = o.fn(h)
            if o.dma is not None:
                ins.then_inc(o.dma.sem, 16)
            elif o.sig is not None:
                ins.then_inc(esem[e], 1)


def _alibi_slopes():
    def pow2(n):
        start = 2.0 ** (-8.0 / n)
        return [start ** (i + 1) for i in range(n)]
    s = pow2(8) + pow2(16)[0::2][:4]
    return np.array(sorted(s, reverse=True), dtype=np.float64).reshape(3, 4)


def _mask_tables(first_in_seq):
    sl = _alibi_slopes()
    k = np.arange(128)[:, None].astype(np.float64)
    q = np.arange(128)[None, :].astype(np.float64)
    out = np.zeros((128, 2, 3, 4, 2, 128), np.float32)
    for g in range(3):
        for hh in range(4):
            s = sl[g, hh] * DIL[g]
            prev = np.where(k >= q, np.exp(-s * (q + 128 - k)), 0.0)
            cur = np.where(k <= q, np.exp(-s * (q - k)), 0.0)
            out[:, 0, g, hh, 0] = prev
            out[:, 0, g, hh, 1] = cur
            out[:, 1, g, hh, 0] = 0.0 if first_in_seq else prev
            out[:, 1, g, hh, 1] = cur
    return out.reshape(128, 2 * 3 * 4 * 256)


def build_nc(debug=False):
    nc = bass.Bass("TRN2", target_bir_lowering=False)

    def din(name, shape):
        return nc.dram_tensor(name, list(shape), F32, kind="ExternalInput").ap()

    xT_d = din("xT", [D, HALO + NT])
    memT_d = din("memT", [D, 256])
    cpack_d = din("cpack", [128, 64 + 4 * 512])
    cmask_d = din("cmask", [128, 24 * 256])
    w_in_d = din("w_in", [D, IN_W])
    w_ba_d = din("w_ba", [256, D])
    w_bs_d = din("w_bs", [512, D])
    w_out_d = din("w_out", [D, D])
    w_qc_d = din("w_qc", [D, 512])
    w_kvc_d = din("w_kvc", [D, 1024])
    w_oc_d = din("w_oc", [512, D])
    w_gu_d = din("w_gu", [D, 2 * D_FF])
    w_dn_d = din("w_dn", [D_FF, D])
    outT_d = nc.dram_tensor("outT", [D, NT], F32, kind="ExternalOutput").ap()
    dbg_d = {}
    if debug:
        for nm, shp in (("d_q", [128, 6 * 2048]), ("d_y", [128, 2 * 2048]), ("d_h1", [128, 8 * 2048]),
                        ("d_h2", [128, 8 * 2048])):
            dbg_d[nm] = nc.dram_tensor(nm, shp, F32, kind="ExternalOutput").ap()

    S = Sched()
    st = ExitStack()
    with st:
        ARENA = 52992
        arena = st.enter_context(nc.sbuf_tensor("arena", [128, ARENA], F32))
        psum = [st.enter_context(nc.psum_tensor("ps%d" % i, [128, 512], F32)) for i in range(8)]
        pbuf = [Buf("ps%d" % i) for i in range(8)]
        esem = {e: st.enter_context(nc.semaphore("s_" + e)) for e in ENGS}
        pcount = [0]

        def nbank():
            i = pcount[0] % 8
            pcount[0] += 1
            assert pbuf[i].last_write is None or len(pbuf[i].reads) > 0, "PSUM bank %d re-used before being read" % i
            return psum[i][:, :], pbuf[i]

        def fv(off, n):
            assert off % 4 == 0 and off // 4 + n <= ARENA, (off, n)
            return arena[:, off // 4: off // 4 + n]

        def bv(off, n):
            assert off % 4 == 0 and n % 2 == 0 and off // 4 + n // 2 <= ARENA, (off, n)
            return arena[:, off // 4: off // 4 + n // 2].bitcast(BF16)

        KB = 1024
        o = 0
        cpack = fv(o, 64 + 4 * 512)
        cvec = fv(o, 56); o += 256
        gsgu = fv(o, 512); o += 2 * KB
        bsgu = fv(o, 512); o += 2 * KB
        bsg16 = bv(o, 512)
        wspf = fv(o, 512); o += 2 * KB
        trilf = fv(o, 512); o += 2 * KB
        ones = bv(o, 128); o += 256
        wspT = bv(o, 512); o += 1 * KB
        cmask = bv(o, 24 * 256); o += 12 * KB
        kmT = bv(o, 4 * 256).rearrange("p (h m) -> p h m", h=4); o += 2 * KB
        vm = bv(o, 2 * 512).rearrange("p (c f) -> p c f", c=2); o += 2 * KB
        ssq = fv(o, 8); o += 32
        rsv = fv(o, 8); o += 32
        assert o <= 26 * KB, o
        o = 26 * KB
        RING_SLOT = 8 * KB
        ring = [bv(o + i * RING_SLOT, 4096) for i in range(4)]
        ring_buf = [Buf("ring%d" % i) for i in range(4)]
        ring_sem = [S.new_dmasem("ring%d" % i) for i in range(4)]
        o += 4 * RING_SLOT
        YT_OFF = o
        yT = bv(o, 2 * 2048).rearrange("p (c t) -> p c t", c=2); o += 8 * KB
        P0 = o
        B_const = Buf("const")
        B_cmask = Buf("cmask")
        B_yT = Buf("yT")

        for d in S.dmasems:
            pass
        misc_sems = {}

        def dsem(name):
            if name not in misc_sems:
                misc_sems[name] = S.new_dmasem(name)
            return misc_sems[name]

        rcount = [0]

        def load_w(src, kch, ncols):
            assert kch * ncols <= 4096
            i = rcount[0] % 4
            rcount[0] += 1
            view = ring[i][:, 0:kch * ncols].rearrange("p (c f) -> p c f", c=kch)
            srcv = src.rearrange("(c p) f -> p c f", p=128)
            S.op("pool", lambda h, v=view, s=srcv: h.dma_start(out=v, in_=s), writes=[ring_buf[i]], dma=ring_sem[i])
            return view, ring_buf[i]

        def load_w_multi(pieces):
            i = rcount[0] % 4
            rcount[0] += 1
            off = 0
            views = []
            for src, kch, ncols in pieces:
                view = ring[i][:, off:off + kch * ncols].rearrange("p (c f) -> p c f", c=kch)
                off += kch * ncols
                assert off <= 4096
                srcv = src.rearrange("(c p) f -> p c f", p=128)
                S.op("pool", lambda h, v=view, s_=srcv: h.dma_start(out=v, in_=s_), writes=[ring_buf[i]], dma=ring_sem[i])
                views.append(view)
            return views, ring_buf[i]

        def dump(name, view):
            if not debug:
                return
            dt = view.dtype
            dr = nc.dram_tensor(name, list(view.shape), dt, kind="ExternalOutput").ap()
            S.barrier()
            S.op("sp", lambda h: h.dma_start(out=dr, in_=view), dma=dsem("dbg"))
            S.barrier()

        def mm(out, lhsT, rhs, start, stop, reads, wbuf):
            S.op("pe", lambda h: h.matmul(out, lhsT=lhsT, rhs=rhs, start=start, stop=stop), reads=reads, writes=[wbuf])

        def act(out, in_, func, reads, writes, **kw):
            S.op("act", lambda h: h.activation(out=out, in_=in_, func=func, **kw), reads=reads, writes=writes)

        def dve(fn, reads, writes):
            S.op("dve", fn, reads=reads, writes=writes)

        ev_rr = [0]

        def evac_copy(out, in_, reads, writes):
            ev_rr[0] += 1
            if ev_rr[0] % 2:
                S.op("act", lambda h: h.activation(out=out, in_=in_, func=AF.Copy), reads=reads, writes=writes)
            else:
                S.op("dve", lambda h: h.tensor_copy(out=out, in_=in_), reads=reads, writes=writes)

        cs = dsem("const")
        B_ones = Buf("ones")
        S.op("dve", lambda h: h.memset(ones, 1.0), writes=[B_ones])
        S.op("dve", lambda h: h.memset(ssq, 0.0), writes=[B_ones])

        def load_consts():
            S.op("sp", lambda h: h.dma_start(out=cpack, in_=cpack_d), writes=[B_const], dma=cs)
        consts_loaded = [False]
        G_MIX, G_CROSS, G_MEM, G_FFN, G_FINAL, B_GATE = 0, 8, 16, 24, 32, 40

        def rmsnorm_fm(src_all, src_fn, src_bufs, ntok, gcol, dst_fn, dst_bufs, sq, sq_buf, rs, rs_buf, nch=8, dim=D, pool_chunks=(), part=None):
            if part in (None, "sq"):
                act(sq[:, :, 0:ntok], src_all, AF.Square, src_bufs, [sq_buf])
            if part == "sq":
                return
            ps, pb = nbank()
            for c in range(nch):
                mm(ps[:, 0:ntok], ones, sq[:, c, 0:ntok], c == 0, c == nch - 1, [sq_buf, B_ones], pb)
            act(rs[:, 0:ntok], ps[:, 0:ntok], AF.Ln, [pb], [rs_buf], scale=1.0 / dim, bias=EPS)
            act(rs[:, 0:ntok], rs[:, 0:ntok], AF.Exp, [rs_buf], [rs_buf], scale=-0.5)
            if pool_chunks:
                npc = len(pool_chunks)
                nd = nch - npc
                rb = rs[:, 0:ntok].unsqueeze(1)
                dst_all = dst_fn(None)
                S.op("dve", lambda h: h.tensor_tensor(out=dst_all[:, 0:nd, :], in0=src_all[:, 0:nd, :],
                                                      in1=rb.broadcast_to([128, nd, ntok]), op=ALU.mult),
                     reads=list(src_bufs) + [rs_buf], writes=dst_bufs)
                S.op("pool", lambda h: h.tensor_tensor(out=dst_all[:, nd:nch, :], in0=src_all[:, nd:nch, :],
                                                       in1=rb.broadcast_to([128, npc, ntok]), op=ALU.mult),
                     reads=list(src_bufs) + [rs_buf], writes=dst_bufs)
            else:
                for c in range(nch):
                    dve(lambda h, c=c: h.scalar_tensor_tensor(out=dst_fn(c), in0=src_fn(c), scalar=cvec[:, gcol + c:gcol + c + 1],
                                                              in1=rs[:, 0:ntok], op0=ALU.mult, op1=ALU.mult),
                        list(src_bufs) + [rs_buf, B_const], dst_bufs)

        o = P0
        QT_OFF = o
        qT = bv(o, 6 * 2048).rearrange("p (c t) -> p c t", c=6); o += 24 * KB
        kT = []
        for g in range(3):
            L = GH[g] + NT
            kT.append(bv(o, 2 * L).rearrange("p (c t) -> p c t", c=2)); o += 4 * L
        NBLK = (17, 20, 32)
        Vt = []
        for g in range(3):
            Vt.append(bv(o, NBLK[g] * 256).rearrange("p (b f) -> p b f", b=NBLK[g])); o += NBLK[g] * 512
        aTf = bv(o, 8 * 2048).rearrange("p (c t) -> p c t", c=8); ACC_OFF = o; o += 32 * KB
        XS_OFF = o
        rsb = [fv(o + i * KB, 256) for i in range(4)]; o += 4 * KB
        XS3_OFF = o; o += 8 * KB
        SQ2_OFF = o; o += 4 * KB
        assert o <= ARENA * 4, o
        XT = 256
        xs = [fv(off_, 8 * XT).rearrange("p (c t) -> p c t", c=8) for off_ in (QT_OFF, QT_OFF + 8 * KB, YT_OFF, XS3_OFF)]
        sqs = [bv(off_, 8 * XT).rearrange("p (c t) -> p c t", c=8) for off_ in (QT_OFF + 16 * KB, QT_OFF + 20 * KB, SQ2_OFF)]
        B_qT = [Buf("qT%d" % c) for c in range(6)]
        B_kT = [[Buf("kT%d_%d" % (g, c)) for c in range(2)] for g in range(3)]
        B_V = [Buf("V%d" % g) for g in range(3)]
        B_aT = [Buf("aT%d" % t) for t in range(4)]
        B_xs = [Buf("xs%d" % i) for i in range(4)]
        B_sqs = [Buf("sqs%d" % i) for i in range(3)]
        B_rs = [Buf("rs%d" % i) for i in range(4)]
        xs_sem = [dsem("xs%d" % i) for i in range(4)]
        xTv = xT_d.rearrange("(c p) t -> p c t", p=128)

        def load_w_g(src, ncols):
            wv, wb = load_w(src, 8, ncols)
            for c in range(8):
                dve(lambda h, c=c, wv=wv: h.tensor_scalar(out=wv[:, c, :], in0=wv[:, c, :], scalar1=cvec[:, G_MIX + c:G_MIX + c + 1],
                                                          scalar2=None, op0=ALU.mult), [wb, B_const], [wb])
            return wv, wb

        def kproj(sb, t, wblk):
            for fc in range(6, 12):
                g, c2 = divmod(fc - 6, 2)
                if sb == 1:
                    tok0, ntok = t * 512, 512
                else:
                    lo = max(t * 512, 2048 - GH[g])
                    if lo >= (t + 1) * 512:
                        continue
                    tok0, ntok = lo, (t + 1) * 512 - lo
                wv, wb = wblk[fc // 4]
                fo = (fc % 4) * 128
                ps, pb = nbank()
                for c in range(8):
                    mm(ps[:, 0:ntok], wv[:, c, fo:fo + 128], aTf[:, c, tok0:tok0 + ntok],
                       c == 0, c == 7, [wb, B_aT[t]], pb)
                dst0 = (tok0 - (2048 - GH[g])) if sb == 0 else GH[g] + tok0
                evac_copy(kT[g][:, c2, dst0:dst0 + ntok], ps[:, 0:ntok], [pb], [B_kT[g][c2]])

        load_consts()
        dve(lambda h: h.tensor_tensor(out=wspT, in0=wspf, in1=trilf, op=ALU.mult), [B_const], [B_const])
        dve(lambda h: h.tensor_copy(out=bsg16, in_=bsgu), [B_const], [B_const])
        xcount = [0]
        for sb in range(2):
            wblk = {}
            for b in ((0, 1, 2) if sb == 1 else (1, 2)):
                wblk[b] = load_w_g(w_in_d[:, b * 512:(b + 1) * 512], 512)
            NXT = 2048 // XT
            pend = {}

            def stA(j):
                i = xcount[0] % 4
                i3 = xcount[0] % 3
                xcount[0] += 1
                t0 = sb * 2048 + j * XT
                S.op("sp", lambda h, i=i, t0=t0: h.dma_start(out=xs[i], in_=xTv[:, :, t0:t0 + XT]),
                     writes=[B_xs[i]], dma=xs_sem[i])
                act(sqs[i3][:, :, 0:XT], xs[i], AF.Square, [B_xs[i]], [B_sqs[i3]])
                ps, pb = nbank()
                for c in range(8):
                    mm(ps[:, 0:XT], ones, sqs[i3][:, c, 0:XT], c == 0, c == 7, [B_sqs[i3], B_ones], pb)
                pend[j] = (i, ps, pb)

            def stB(j):
                i, ps, pb = pend.pop(j)
                t = (j * XT) // 512
                rs = rsb[i]
                act(rs[:, 0:XT], ps[:, 0:XT], AF.Ln, [pb], [B_rs[i]], scale=1.0 / D, bias=EPS)
                act(rs[:, 0:XT], rs[:, 0:XT], AF.Exp, [B_rs[i]], [B_rs[i]], scale=-0.5)
                rb = rs[:, 0:XT].unsqueeze(1)
                dst = aTf[:, :, j * XT:(j + 1) * XT]
                S.op("dve", lambda h: h.tensor_tensor(out=dst[:, 0:5, :], in0=xs[i][:, 0:5, :],
                                                      in1=rb.broadcast_to([128, 5, XT]), op=ALU.mult),
                     reads=[B_xs[i], B_rs[i]], writes=[B_aT[t]])
                S.op("pool", lambda h: h.tensor_tensor(out=dst[:, 5:8, :], in0=xs[i][:, 5:8, :],
                                                       in1=rb.broadcast_to([128, 3, XT]), op=ALU.mult),
                     reads=[B_xs[i], B_rs[i]], writes=[B_aT[t]])

            stA(0)
            for j in range(NXT):
                if j + 1 < NXT:
                    stA(j + 1)
                stB(j)
                if (j + 1) % (512 // XT) == 0:
                    t = (j + 1) // (512 // XT) - 1
                    if t >= 1:
                        kproj(sb, t - 1, wblk)
            kproj(sb, 3, wblk)
            if sb == 1:
                for t in range(4):
                    for fc in range(6):
                        wv, wb = wblk[fc // 4]
                        fo = (fc % 4) * 128
                        ps, pb = nbank()
                        for c in range(8):
                            mm(ps, wv[:, c, fo:fo + 128], aTf[:, c, t * 512:(t + 1) * 512], c == 0, c == 7, [wb] + B_aT, pb)
                        evac_copy(qT[:, fc, t * 512:(t + 1) * 512], ps, [pb], [B_qT[fc]])
            for g in range(3):
                wv, wb = load_w_g(w_in_d[:, 1536 + g * 256:1536 + (g + 1) * 256], 256)
                blocks = []
                if g == 0:
                    if sb == 0:
                        blocks.append((0, slice(1920, 2048)))
                    else:
                        blocks += [(1 + n, slice(n * 128, (n + 1) * 128)) for n in range(16)]
                elif g == 1:
                    if sb == 0:
                        blocks += [(r, slice(1536 + r, 2048, 4)) for r in range(4)]
                    else:
                        blocks += [(4 + n1 * 4 + r, slice(512 * n1 + r, 512 * (n1 + 1), 4)) for n1 in range(4) for r in range(4)]
                else:
                    blocks += [(sb * 16 + r, slice(r, 2048, 16)) for r in range(16)]
                for p in range(0, len(blocks), 2):
                    grp = blocks[p:p + 2]
                    ps, pb = nbank()
                    for bi, (blk, sl) in enumerate(grp):
                        for c in range(8):
                            mm(ps[:, bi * 256:(bi + 1) * 256], aTf[:, c, sl], wv[:, c, :], c == 0, c == 7, [wb] + B_aT, pb)
                    b0 = grp[0][0]
                    n = len(grp)
                    evac_copy(Vt[g][:, b0:b0 + n, :], ps[:, 0:n * 256].rearrange("p (b f) -> p b f", b=n), [pb], [B_V[g]])

        S.op("pool", lambda h: h.dma_start(out=cmask, in_=cmask_d), writes=[B_cmask], dma=dsem("cmask"))
        if debug:
            S.barrier()
            dq = dsem("dbg")
            stg = fv(XS_OFF, 2048)
            B_stg = Buf("stg")
            for c in range(6):
                dve(lambda h, c=c: h.tensor_copy(out=stg, in_=qT[:, c, :]), [B_qT[c]], [B_stg])
                S.op("sp", lambda h, c=c: h.dma_start(out=dbg_d["d_q"][:, c * 2048:(c + 1) * 2048], in_=stg),
                     reads=[B_stg], dma=dq)
        dump("d_cmask", cmask)
        dump("d_kT0", kT[0].rearrange("p c t -> p (c t)"))
        dump("d_V0", Vt[0].rearrange("p b f -> p (b f)"))
        S.barrier()

        acc = fv(ACC_OFF, 2 * 2 * 2048).rearrange("p (c n t) -> p c n t", c=2, n=2)
        o = QT_OFF
        o = XS_OFF
        NEB = 3
        ET = [[bv(o + (2 * i + hh) * KB, 512) for hh in range(2)] for i in range(NEB)]; o += 2 * NEB * KB
        EM = [[bv(o + (2 * i + hh) * KB, 512) for hh in range(2)] for i in range(NEB)]; o += 2 * NEB * KB
        assert o <= ARENA * 4, o
        B_ET = [[ScrBuf("ET%d%d" % (i, hh)) for hh in range(2)] for i in range(NEB)]
        B_EM = [[ScrBuf("EM%d%d" % (i, hh)) for hh in range(2)] for i in range(NEB)]
        B_acc = [ScrBuf("acc%d" % c) for c in range(2)]
        cmv = cmask.rearrange("p (v g h x) -> p v g h x", v=2, g=3, h=4)

        def qsl(g, n):
            if g == 0:
                return slice(n * 128, (n + 1) * 128)
            if g == 1:
                n1, r = divmod(n, 4)
                return slice(512 * n1 + r, 512 * (n1 + 1), 4)
            return slice(n, 2048, 16)

        def ksl(g, n, kb):
            s_ = qsl(g, n)
            off = GH[g] if kb == 1 else GH[g] - 128 * DIL[g]
            return slice(s_.start + off, s_.stop + off, s_.step)

        def vblk(g, n, kb):
            if g == 0:
                return n + kb
            if g == 1:
                n1, r = divmod(n, 4)
                return (n1 + kb) * 4 + r
            return kb * 16 + n

        def is_halo(g, n):
            return (g == 0 and n == 0) or (g == 1 and n < 4) or g == 2

        iters = [(g, hp, npair) for g in range(3) for hp in range(2) for npair in range(8)]
        NI = len(iters)
        sps_of = {}

        def st_S(i):
            g, hp, npair = iters[i]
            ch = 2 * g + hp
            nn = (2 * npair, 2 * npair + 1)
            sps = []
            for hh in range(2):
                bk = (2 * i + hh) % 6
                ps, pb = psum[bk][:, :], pbuf[bk]
                sps.append((ps, pb))
                pr = slice(hh * 64, (hh + 1) * 64)
                for qi, n in enumerate(nn):
                    for kb in range(2):
                        col = (qi * 2 + kb) * 128
                        mm(ps[:, col:col + 128], kT[g][pr, hp, ksl(g, n, kb)], qT[pr, ch, qsl(g, n)],
                           True, True, [B_kT[g][hp], B_qT[ch]], pb)
            sps_of[i] = sps

        def st_E(i):
            g, hp, npair = iters[i]
            nn = (2 * npair, 2 * npair + 1)
            bi = i % NEB
            for hh in range(2):
                ps, pb = sps_of[i][hh]
                act(ET[bi][hh], ps, AF.Exp, [pb], [B_ET[bi][hh]], scale=0.125)
                vs = [1 if is_halo(g, n) else 0 for n in nn]
                eng = "dve" if hh == 0 else "pool"
                if vs[0] == vs[1]:
                    S.op(eng, lambda h, bi=bi, hh=hh, v=vs[0], g=g, hp=hp: h.tensor_tensor(
                        out=EM[bi][hh].rearrange("p (q x) -> p q x", q=2), in0=ET[bi][hh].rearrange("p (q x) -> p q x", q=2),
                        in1=cmv[:, v, g, 2 * hp + hh, :].unsqueeze(1).broadcast_to([128, 2, 256]), op=ALU.mult),
                        reads=[B_ET[bi][hh], B_cmask], writes=[B_EM[bi][hh]])
                else:
                    for qi, n in enumerate(nn):
                        S.op(eng, lambda h, bi=bi, hh=hh, qi=qi, v=vs[qi], g=g, hp=hp: h.tensor_tensor(
                            out=EM[bi][hh][:, qi * 256:(qi + 1) * 256], in0=ET[bi][hh][:, qi * 256:(qi + 1) * 256],
                            in1=cmv[:, v, g, 2 * hp + hh, :], op=ALU.mult),
                            reads=[B_ET[bi][hh], B_cmask], writes=[B_EM[bi][hh]])

        def st_P(i):
            g, hp, npair = iters[i]
            nn = (2 * npair, 2 * npair + 1)
            bi = i % NEB
            bk = 6 + (i % 2)
            ps, pb = psum[bk][:, :], pbuf[bk]
            for hh in range(2):
                pr = slice(hh * 64, (hh + 1) * 64)
                hcol = slice((2 * hp + hh) * 64, (2 * hp + hh + 1) * 64)
                for qi, n in enumerate(nn):
                    for kb in range(2):
                        e = EM[bi][hh][:, (qi * 2 + kb) * 128:(qi * 2 + kb + 1) * 128]
                        mm(ps[pr, qi * 128:(qi + 1) * 128], Vt[g][:, vblk(g, n, kb), hcol], e,
                           kb == 0, kb == 1, [B_V[g], B_EM[bi][hh]], pb)
                    for kb in range(2):
                        e = EM[bi][hh][:, (qi * 2 + kb) * 128:(qi * 2 + kb + 1) * 128]
                        mm(ps[pr, 256 + qi * 128:256 + (qi + 1) * 128], ones[:, 0:64], e,
                           kb == 0, kb == 1, [B_ones, B_EM[bi][hh]], pb)
            n0 = nn[0]
            if g == 0:
                dst = acc[:, hp, :, n0 * 128:(n0 + 2) * 128].rearrange("p n (q i) -> p n q i", q=2)
            elif g == 1:
                n1, r = divmod(n0, 4)
                dst = acc[:, hp, :, 512 * n1:512 * (n1 + 1)].rearrange("p n (i r) -> p n r i", r=4)[:, :, r:r + 2, :]
            else:
                dst = acc[:, hp, :, :].rearrange("p n (i r) -> p n r i", r=16)[:, :, n0:n0 + 2, :]
            src = ps.rearrange("p (n q i) -> p n q i", n=2, q=2)
            if g == 0:
                act(dst, src, AF.Copy, [pb], [B_acc[hp]])
            else:
                dve(lambda h, dst=dst, src=src: h.tensor_tensor(out=dst, in0=src, in1=dst, op=ALU.add),
                    [pb, B_acc[hp]], [B_acc[hp]])

        st_S(0)
        st_S(1)
        st_E(0)
        for i in range(NI):
            if i + 2 < NI:
                st_S(i + 2)
            if i + 1 < NI:
                st_E(i + 1)
            st_P(i)
        dump("d_acc", acc.rearrange("p c n t -> p (c n t)"))
        for hp in range(2):
            act(acc[:, hp, 1, :], acc[:, hp, 1, :], AF.Ln, [B_acc[hp]], [B_acc[hp]])
            act(acc[:, hp, 1, :], acc[:, hp, 1, :], AF.Exp, [B_acc[hp]], [B_acc[hp]], scale=-1.0)
            dve(lambda h, hp=hp: h.tensor_tensor(out=yT[:, hp, :], in0=acc[:, hp, 0, :], in1=acc[:, hp, 1, :], op=ALU.mult),
                [B_acc[hp]], [B_yT])
        if debug:
            S.barrier()
            stg = fv(XS_OFF + 8 * KB, 2048)
            for c in range(2):
                dve(lambda h, c=c: h.tensor_copy(out=stg, in_=yT[:, c, :]), [B_yT], [B_stg])
                S.op("sp", lambda h, c=c: h.dma_start(out=dbg_d["d_y"][:, c * 2048:(c + 1) * 2048], in_=stg),
                     reads=[B_stg], dma=dq)
            S.barrier()

        o = P0
        hT = fv(o, 8 * 2048).rearrange("p (c t) -> p c t", c=8); o += 64 * KB
        HT = 1024
        nT = bv(o, 8 * HT).rearrange("p (c t) -> p c t", c=8); o += 16 * KB
        sq3 = bv(o, 8 * 512).rearrange("p (c t) -> p c t", c=8); o += 8 * KB
        rs3 = [fv(o + i * 2 * KB, 512) for i in range(2)]; o += 4 * KB
        SCR = o
        B_hT = [[Buf("hT%d_%d" % (c, t)) for t in range(4)] for c in range(8)]
        B_nT = [Buf("nT%d" % t) for t in range(2)]
        B_sq3 = Buf("sq3")
        B_rs3 = [Buf("rs3_%d" % i) for i in range(2)]
        hs = dsem("hT")
        p12_bufs = B_qT + [b_ for g_ in range(3) for b_ in B_kT[g_]] + B_V + B_aT
        hprev = None
        for t in range(4):
            hprev = S.op("sp", lambda h, t=t: h.dma_start(out=hT[:, :, t * 512:(t + 1) * 512],
                                                           in_=xTv[:, :, HALO + t * 512:HALO + (t + 1) * 512]),
                         writes=[B_hT[c][t] for c in range(8)] + p12_bufs, dma=hs,
                         extra_deps=([hprev] if (hprev is not None and t >= 1) else []))
        S.op("sp", lambda h: h.nop(), writes=[S.fence_buf])

        B_kv = Buf("kvmem")

        def kv_mem(o):
            memf = fv(o, 8 * 256).rearrange("p (c t) -> p c t", c=8); o += 8 * KB
            mnT = bv(o, 8 * 256).rearrange("p (c t) -> p c t", c=8); o += 4 * KB
            assert o <= ARENA * 4, o
            B_memf = ScrBuf("memf"); B_mnT = ScrBuf("mnT")
            S.op("sp", lambda h: h.dma_start(out=memf, in_=memT_d.rearrange("(c p) t -> p c t", p=128)),
                 writes=[B_memf], dma=dsem("memf"))
            rmsnorm_fm(memf, lambda c: memf[:, c, :], [B_memf], 256, G_MEM, lambda c: mnT[:, c, :], [B_mnT],
                       sq3, B_sq3, rs3[0], B_rs3[0])
            wv, wb = load_w(w_kvc_d[:, 0:512], 8, 512)
            for hd in range(4):
                ps, pb = nbank()
                for c in range(8):
                    mm(ps[:, 0:256], wv[:, c, hd * 128:(hd + 1) * 128], mnT[:, c, :], c == 0, c == 7, [wb, B_mnT], pb)
                evac_copy(kmT[:, hd, :], ps[:, 0:256], [pb], [B_kv])
            wv, wb = load_w(w_kvc_d[:, 512:1024], 8, 512)
            for mc in range(2):
                ps, pb = nbank()
                for c in range(8):
                    mm(ps, mnT[:, c, mc * 128:(mc + 1) * 128], wv[:, c, :], c == 0, c == 7, [wb, B_mnT], pb)
                evac_copy(vm[:, mc, :], ps, [pb], [B_kv])

        def norm_tile(H, tt, gcol, part=None):
            t = 2 * H + tt
            rmsnorm_fm(hT[:, :, t * 512:(t + 1) * 512], lambda c, t=t: hT[:, c, t * 512:(t + 1) * 512], [B_hT[c][t] for c in range(8)], 512, gcol,
                       lambda c, tt=tt: nT[:, c, tt * 512:(tt + 1) * 512], [B_nT[tt]],
                       sq3, B_sq3, rs3[tt], B_rs3[tt], part=part)

        def norm_half(H, gcol):
            for tt in range(2):
                norm_tile(H, tt, gcol)

        def resid_add(ps, pb, m, t):
            dve(lambda h: h.tensor_tensor(out=hT[:, m, t * 512:(t + 1) * 512], in0=ps, in1=hT[:, m, t * 512:(t + 1) * 512],
                                          op=ALU.add), [pb, B_hT[m][t]], [B_hT[m][t]])

        import os
        K_FENCE = os.environ.get("K_FENCE", "1") == "1"
        K_HOIST = os.environ.get("K_HOIST", "1") == "1"
        K_PIPE = os.environ.get("K_PIPE", "1") == "1"

        def fence():
            if K_FENCE:
                S.op("sp", lambda h: h.nop(), writes=[S.fence_buf])
            else:
                S.barrier()

        norm_half(0, G_MIX)
        for H in range(2):
            tok0 = H * HT
            o = SCR
            uT = bv(o, 4 * HT).rearrange("p (c t) -> p c t", c=4); o += 8 * KB
            ysT = bv(o, 4 * HT).rearrange("p (c t) -> p c t", c=4); o += 8 * KB
            vg = fv(o, 8 * 512).rearrange("p (b f) -> p b f", b=8)
            mgT = bv(o, 8 * HT).rearrange("p (c t) -> p c t", c=8); o += 16 * KB
            NVN = 4
            vn = [bv(o + i * KB, 512) for i in range(NVN)]; o += NVN * KB
            mxb = [fv(o + i * 2 * KB, 512) for i in range(2)]; o += 4 * KB
            gt = [bv(o + i * KB, 512) for i in range(4)]; o += 4 * KB
            t1b = [fv(o + i * 2 * KB, 512) for i in range(2)]; o += 4 * KB
            assert o <= ARENA * 4, o
            B_uT = ScrBuf("uT"); B_ysT = ScrBuf("ysT"); B_vg = ScrBuf("vg"); B_vn = [ScrBuf("vn%d" % i) for i in range(NVN)]
            B_mx = [ScrBuf("mx0"), ScrBuf("mx1")]; B_mg = [ScrBuf("mg%d" % t) for t in range(2)]
            B_gt = [ScrBuf("gt%d" % i) for i in range(4)]; B_t1 = [ScrBuf("t1_0"), ScrBuf("t1_1")]
            B_ssq = Buf("ssq")
            if H == 1 and not K_HOIST:
                norm_half(1, G_MIX)
            dve(lambda h: h.memset(ssq, 0.0), [], [B_ssq])
            wvu, wbu = load_w(w_in_d[:, 2304:2816], 8, 512)
            wvv, wbv = load_w(w_in_d[:, 2816:3328], 8, 512)
            for tt in range(2):
                for blk in range(4 * tt, 4 * tt + 4):
                    bo = (blk * 128) % 512
                    ps, pb = nbank()
                    for c in range(8):
                        mm(ps, nT[:, c, tt * 512 + bo:tt * 512 + bo + 128], wvv[:, c, :], c == 0, c == 7, [wbv, B_nT[tt]], pb)
                    act(vg[:, blk, :], ps, AF.Gelu, [pb], [B_vg])
                    act(t1b[blk % 2], vg[:, blk, :], AF.Square, [B_vg], [B_t1[blk % 2], B_ssq], accum_out=ssq[:, blk:blk + 1])
                for fc in range(4):
                    ps, pb = nbank()
                    for c in range(8):
                        mm(ps, wvu[:, c, fc * 128:(fc + 1) * 128], nT[:, c, tt * 512:(tt + 1) * 512], c == 0, c == 7,
                           [wbu, B_nT[tt]], pb)
                    act(uT[:, fc, tt * 512:(tt + 1) * 512], ps, AF.Gelu, [pb], [B_uT])
            act(rsv, ssq, AF.Ln, [B_ssq], [B_ssq], scale=1.0 / 512, bias=EPS)
            act(rsv, rsv, AF.Exp, [B_ssq], [B_ssq], scale=-0.5)
            for blk in range(8):
                i = blk % NVN
                dve(lambda h, blk=blk, i=i: h.scalar_tensor_tensor(out=vn[i], in0=vg[:, blk, :], scalar=rsv[:, blk:blk + 1],
                                                                    in1=gsgu, op0=ALU.mult, op1=ALU.mult),
                    [B_vg, B_ssq, B_const], [B_vn[i]])
                ps, pb = nbank()
                mm(ps, ones[0:1, 0:128], bsg16[0:1, :], True, False, [B_ones, B_const], pb)
                for gi in range(4):
                    mm(ps[:, gi * 128:(gi + 1) * 128], vn[i][:, gi * 128:(gi + 1) * 128], wspT[:, gi * 128:(gi + 1) * 128],
                       False, gi == 3, [B_vn[i], B_const], pb)
                dve(lambda h, ps=ps, blk=blk: h.tensor_tensor(
                    out=ysT[:, :, blk * 128:(blk + 1) * 128], in0=ps.rearrange("p (g t) -> p g t", g=4),
                    in1=uT[:, :, blk * 128:(blk + 1) * 128], op=ALU.mult), [pb, B_uT], [B_ysT])
            for m in range(8):
                fsl = slice(m * 128, (m + 1) * 128)
                (wg0, wg1, wba, wbs), wsb = load_w_multi([
                    (w_in_d[:, 3328 + m * 128:3328 + (m + 1) * 128], 8, 128),
                    (w_in_d[:, 4352 + m * 128:4352 + (m + 1) * 128], 8, 128),
                    (w_ba_d[:, fsl], 2, 128),
                    (w_bs_d[:, fsl], 4, 128)])
                for tt in range(2):
                    ts = slice(tt * 512, (tt + 1) * 512)
                    tsg = slice(tok0 + tt * 512, tok0 + (tt + 1) * 512)
                    gi0 = (2 * tt) % 4
                    for k, (wv_, boff) in enumerate(((wg0, 0), (wg1, 8))):
                        ps, pb = nbank()
                        for c in range(8):
                            mm(ps, wv_[:, c, :], nT[:, c, ts], c == 0, c == 7, [wsb, B_nT[tt]], pb)
                        act(gt[gi0 + k], ps, AF.Sigmoid, [pb, B_const], [B_gt[gi0 + k]],
                            bias=cvec[:, B_GATE + boff + m:B_GATE + boff + m + 1])
                    ps, pb = nbank()
                    for c in range(2):
                        mm(ps, wba[:, c, :], yT[:, c, tsg], c == 0, c == 1, [wsb, B_yT], pb)
                    dve(lambda h, ps=ps, i=tt, gi0=gi0: h.tensor_tensor(out=t1b[i], in0=ps, in1=gt[gi0], op=ALU.mult),
                        [pb, B_gt[gi0]], [B_t1[tt]])
                    ps, pb = nbank()
                    for c in range(4):
                        mm(ps, wbs[:, c, :], ysT[:, c, ts], c == 0, c == 3, [wsb, B_ysT], pb)
                    dve(lambda h, ps=ps, i=tt, gi0=gi0: h.tensor_tensor(out=mxb[i], in0=ps, in1=gt[gi0 + 1], op=ALU.mult),
                        [pb, B_gt[gi0 + 1]], [B_mx[tt]])
                    dve(lambda h, i=tt, m=m, ts=ts: h.tensor_tensor(out=mgT[:, m, ts], in0=t1b[i], in1=mxb[i], op=ALU.add),
                        [B_t1[tt], B_mx[tt]], [B_mg[tt], B_vg])
            wo = [load_w(w_out_d[:, mg * 512:(mg + 1) * 512], 8, 512) for mg in range(2)]
            for tt in range(2):
                for m in range(8):
                    wv, wb = wo[m // 4]
                    mc = m % 4
                    ps, pb = nbank()
                    for c in range(8):
                        mm(ps, wv[:, c, mc * 128:(mc + 1) * 128], mgT[:, c, tt * 512:(tt + 1) * 512], c == 0, c == 7,
                           [wb, B_mg[tt]], pb)
                    resid_add(ps, pb, m, 2 * H + tt)
            fence()
            o = SCR
            qcT = bv(o, 4 * HT).rearrange("p (c t) -> p c t", c=4); o += 8 * KB
            ocT = bv(o, 4 * HT).rearrange("p (c t) -> p c t", c=4); o += 8 * KB
            NEC = 3
            ec = [bv(o + i * 2 * KB, 1024).rearrange("p (c t) -> p c t", c=2) for i in range(NEC)]; o += NEC * 2 * KB
            rd = [fv(o + i * 2 * KB, 512) for i in range(2)]; o += 4 * KB
            B_qc = [ScrBuf("qcT0"), ScrBuf("qcT1")]; B_oc = [ScrBuf("ocT0"), ScrBuf("ocT1")]
            B_ec = [ScrBuf("ec%d" % i) for i in range(NEC)]; B_rd = [ScrBuf("rd0"), ScrBuf("rd1")]
            norm_half(H, G_CROSS)
            wv, wb = load_w(w_qc_d, 8, 512)
            for tt in range(2):
                for hd in range(4):
                    ps, pb = nbank()
                    for c in range(8):
                        mm(ps, wv[:, c, hd * 128:(hd + 1) * 128], nT[:, c, tt * 512:(tt + 1) * 512], c == 0, c == 7,
                           [wb, B_nT[tt]], pb)
                    evac_copy(qcT[:, hd, tt * 512:(tt + 1) * 512], ps, [pb], [B_qc[tt]])
                if H == 0 and tt == 0:
                    kv_mem(o)
            woc = [load_w(w_oc_d[:, mg * 512:(mg + 1) * 512], 4, 512) for mg in range(2)]
            cits = [(tt, hd) for tt in range(2) for hd in range(4)]
            csp = {}

            def c_S(i):
                tt, hd = cits[i]
                ts = slice(tt * 512, (tt + 1) * 512)
                lst = []
                for mc in range(2):
                    ps, pb = nbank()
                    mm(ps, kmT[:, hd, mc * 128:(mc + 1) * 128], qcT[:, hd, ts], True, True, [B_kv, B_qc[tt]], pb)
                    lst.append((ps, pb))
                csp[i] = lst

            def c_E(i):
                k = i % NEC
                for mc in range(2):
                    ps, pb = csp[i][mc]
                    act(ec[k][:, mc, :], ps, AF.Exp, [pb], [B_ec[k]], scale=1.0 / math.sqrt(128.0))

            def c_P(i):
                tt, hd = cits[i]
                ts = slice(tt * 512, (tt + 1) * 512)
                k = i % NEC
                j = i % 2
                psn, pbn = nbank()
                for mc in range(2):
                    mm(psn, vm[:, mc, hd * 128:(hd + 1) * 128], ec[k][:, mc, :], mc == 0, mc == 1, [B_kv, B_ec[k]], pbn)
                psd, pbd = nbank()
                for mc in range(2):
                    mm(psd, ones, ec[k][:, mc, :], mc == 0, mc == 1, [B_ones, B_ec[k]], pbd)
                act(rd[j], psd, AF.Ln, [pbd], [B_rd[j]])
                act(rd[j], rd[j], AF.Exp, [B_rd[j]], [B_rd[j]], scale=-1.0)
                dve(lambda h: h.tensor_tensor(out=ocT[:, hd, ts], in0=psn, in1=rd[j], op=ALU.mult),
                    [pbn, B_rd[j]], [B_oc[tt]])

            def c_O(tt):
                for m in range(8):
                    wv, wb = woc[m // 4]
                    mc = m % 4
                    ps, pb = nbank()
                    for c in range(4):
                        mm(ps, wv[:, c, mc * 128:(mc + 1) * 128], ocT[:, c, tt * 512:(tt + 1) * 512], c == 0, c == 3,
                           [wb, B_oc[tt]], pb)
                    resid_add(ps, pb, m, 2 * H + tt)

            if K_PIPE:
                for tt in range(2):
                    b0 = 4 * tt
                    c_S(b0); c_S(b0 + 1); c_E(b0)
                    for i in range(b0, b0 + 4):
                        if i + 2 < b0 + 4:
                            c_S(i + 2)
                        if i + 1 < b0 + 4:
                            c_E(i + 1)
                        c_P(i)
                    c_O(tt)
            else:
                for i in range(8):
                    c_S(i); c_E(i); c_P(i)
                    if i == 3:
                        c_O(0)
                c_O(1)
            fence()
            o = SCR
            hid = bv(o, 22 * HT).rearrange("p (c t) -> p c t", c=22); o += 44 * KB
            sg = [bv(o + i * KB, 512) for i in range(2)]; o += 2 * KB
            assert o <= ARENA * 4, o
            B_hid = [ScrBuf("hid%d" % t) for t in range(2)]
            B_sg = [ScrBuf("sg0"), ScrBuf("sg1")]
            norm_tile(H, 0, G_FFN)
            norm_tile(H, 1, G_FFN, part="sq")
            its = 0
            for jb in range(0, 22, 4):
                nj = min(4, 22 - jb)
                wg, wgb = load_w(w_gu_d[:, jb * 128:(jb + nj) * 128], 8, nj * 128)
                wu, wub = load_w(w_gu_d[:, D_FF + jb * 128:D_FF + (jb + nj) * 128], 8, nj * 128)
                order = [(jj, tt) for jj in range(nj) for tt in range(2)] if jb > 0 else \
                        [(jj, tt) for tt in range(2) for jj in range(nj)]
                for jj, tt in order:
                    if jb == 0 and tt == 0 and jj == 1:
                        norm_tile(H, 1, G_FFN, part="rest")
                    if True:
                        j = jb + jj
                        fs = slice(jj * 128, (jj + 1) * 128)
                        ts = slice(tt * 512, (tt + 1) * 512)
                        i = its % 2
                        its += 1
                        psg, pbg = nbank()
                        for c in range(8):
                            mm(psg, wg[:, c, fs], nT[:, c, ts], c == 0, c == 7, [wgb, B_nT[tt]], pbg)
                        psu, pbu = nbank()
                        for c in range(8):
                            mm(psu, wu[:, c, fs], nT[:, c, ts], c == 0, c == 7, [wub, B_nT[tt]], pbu)
                        act(sg[i], psg, AF.Silu, [pbg], [B_sg[i]])
                        dve(lambda h, i=i, psu=psu, j=j, ts=ts: h.tensor_tensor(out=hid[:, j, ts], in0=psu, in1=sg[i], op=ALU.mult),
                            [pbu, B_sg[i]], [B_hid[tt]])
            def down_group(m, wv, wb, tt):
                ps, pb = nbank()
                for j in range(22):
                    mm(ps, wv[:, j, :], hid[:, j, tt * 512:(tt + 1) * 512], j == 0, j == 21, [wb, B_hid[tt]], pb)
                resid_add(ps, pb, m, 2 * H + tt)

            for m in range(4):
                wv, wb = load_w(w_dn_d[:, m * 128:(m + 1) * 128], 22, 128)
                for tt in range(2):
                    down_group(m, wv, wb, tt)
                if H == 0 and K_HOIST and m < 2:
                    norm_tile(1, m, G_MIX)
            wd = [load_w(w_dn_d[:, m * 128:(m + 1) * 128], 22, 128) for m in range(4, 8)]
            for tt in range(2):
                for mi, m in enumerate(range(4, 8)):
                    down_group(m, wd[mi][0], wd[mi][1], tt)
            fence()
            o = SCR
            ost = [fv(o + i * 16 * KB, 8 * 512).rearrange("p (c t) -> p c t", c=8) for i in range(2)]; o += 32 * KB
            B_ost = [ScrBuf("ost0"), ScrBuf("ost1")]
            osem = [dsem("ost0"), dsem("ost1")]
            outv = outT_d.rearrange("(c p) t -> p c t", p=128)
            for tt in range(2):
                t = 2 * H + tt
                rmsnorm_fm(hT[:, :, t * 512:(t + 1) * 512], lambda c, t=t: hT[:, c, t * 512:(t + 1) * 512], [B_hT[c][t] for c in range(8)], 512, G_FINAL,
                           lambda c, tt=tt: ost[tt][:, c, :], [B_ost[tt]],
                           sq3 if (tt == 0 or H == 0) else nT[:, :, 0:512], B_sq3 if (tt == 0 or H == 0) else B_nT[0],
                           rs3[tt], B_rs3[tt])
                S.op("sp", lambda h, tt=tt, t=t: h.dma_start(out=outv[:, :, t * 512:(t + 1) * 512], in_=ost[tt]),
                     reads=[B_ost[tt]], dma=osem[tt])
            fence()
        S.barrier()

        for d in S.dmasems:
            d.sem = st.enter_context(nc.semaphore("d_" + d.name))
        S.finalize()
        with nc.Block() as block:
            @block.tensor
            def _(h):
                S.emit(esem, h, "pe")

            @block.scalar
            def _(h):
                S.emit(esem, h, "act")

            @block.vector
            def _(h):
                S.emit(esem, h, "dve")

            @block.gpsimd
            def _(h):
                S.emit(esem, h, "pool")

            @block.sync
            def _(h):
                S.emit(esem, h, "sp")
    return nc


def make_in_maps(inputs):
    f = lambda a: np.ascontiguousarray(np.asarray(a, dtype=np.float32))
    x = f(inputs["x"]); mem = f(inputs["mem"])
    vec8 = lambda v: np.asarray(v, np.float32).reshape(8, 128).T
    cvec = np.concatenate([vec8(inputs["g_mix"][0]), vec8(inputs["g_cross"][0]), vec8(inputs["g_mem"][0]),
                           vec8(inputs["g_ffn"][0]), vec8(inputs["g_final"]),
                           np.asarray(inputs["b_gate"][0], np.float32).reshape(16, 128).T], axis=1)
    gsgu_b = np.broadcast_to(np.asarray(inputs["g_sgu"][0], np.float32)[None, :], (128, 512))
    bsgu_b = np.broadcast_to(np.asarray(inputs["b_sgu_spatial"][0], np.float32).reshape(1, 512), (128, 512))
    wsp = np.asarray(inputs["w_sgu_spatial"][0], np.float32)
    wspT = np.transpose(wsp, (2, 0, 1)).reshape(128, 512)
    s_i = np.arange(128)[:, None]; t_i = np.arange(128)[None, :]
    trilm = np.tile((s_i <= t_i).astype(np.float32), (1, 4))
    shared = {
        "cpack": f(np.concatenate([cvec, np.zeros((128, 8), np.float32), gsgu_b, bsgu_b, wspT, trilm], axis=1)),
        "w_in": f(inputs["w_in"][0]), "w_ba": f(inputs["w_branch_attn"][0]), "w_bs": f(inputs["w_branch_sgu"][0]),
        "w_out": f(inputs["w_out"][0]), "w_qc": f(inputs["w_q_cross"][0]), "w_kvc": f(inputs["w_kv_cross"][0]),
        "w_oc": f(inputs["w_o_cross"][0]), "w_gu": f(inputs["w_gate_up"][0]), "w_dn": f(inputs["w_down"][0]),
    }
    masks = [f(_mask_tables(True)), f(_mask_tables(False))]
    maps = []
    for core in range(8):
        b, part = divmod(core, 4)
        t0 = part * NT
        xT = np.zeros((D, HALO + NT), np.float32)
        if part > 0:
            xT[:, 0:HALO] = x[b, t0 - HALO:t0, :].T
        xT[:, HALO:] = x[b, t0:t0 + NT, :].T
        m = dict(shared)
        m["xT"] = xT
        m["memT"] = f(mem[b].T)
        m["cmask"] = masks[0] if part == 0 else masks[1]
        maps.append(m)
    return maps


def kernel(**inputs):
    nc = build_nc()
    maps = make_in_maps(inputs)
    res = run_bass_kernel_spmd(nc, maps, core_ids=list(range(8)))
    out = np.zeros((2, 8192, D), np.float32)
    for core in range(8):
        b, part = divmod(core, 4)
        out[b, part * NT:(part + 1) * NT, :] = res.results[core]["outT"].T
    return out
```

```python
import math
from contextlib import ExitStack

import numpy as np
import concourse.bass as bass
import concourse.mybir as mybir
from concourse.bass_utils import run_bass_kernel_spmd

F32 = mybir.dt.float32
BF16 = mybir.dt.bfloat16
AF = mybir.ActivationFunctionType
ALU = mybir.AluOpType

ENGS = ("pe", "act", "dve", "pool", "sp")

D = 1024
NT = 2048
HALO = 2048
IN_W = 5376
D_FF = 2816
EPS = 1e-6
GH = (128, 512, 2048)
DIL = (1, 4, 16)


class Buf:
    __slots__ = ("name", "last_write", "reads")

    def __init__(self, name):
        self.name = name
        self.last_write = None
        self.reads = []


class ScrBuf(Buf):
    __slots__ = ()


class DmaSem:
    __slots__ = ("sem", "count", "name")

    def __init__(self, name):
        self.name = name
        self.sem = None
        self.count = 0


class Op:
    __slots__ = ("eng", "fn", "deps", "sig", "dma", "dma_val", "waits", "idx")

    def __init__(self, eng, fn, dma=None):
        self.eng = eng
        self.fn = fn
        self.deps = []
        self.sig = None
        self.dma = dma
        self.dma_val = None
        self.waits = None


class Sched:
    def __init__(self):
        self.q = {e: [] for e in ENGS}
        self.dmasems = []
        self.dma_ops = []
        self.fence_buf = Buf("fence")

    def new_dmasem(self, name):
        d = DmaSem(name)
        self.dmasems.append(d)
        return d

    def op(self, eng, fn, reads=(), writes=(), dma=None, extra_deps=()):
        o = Op(eng, fn, dma=dma)
        deps = list(extra_deps)
        if any(isinstance(b, ScrBuf) for b in reads) or any(isinstance(b, ScrBuf) for b in writes):
            reads = list(reads) + [self.fence_buf]
        for b in reads:
            if b.last_write is not None:
                deps.append(b.last_write)
        for b in writes:
            if b.last_write is not None:
                deps.append(b.last_write)
            deps.extend(b.reads)
        o.idx = len(self.q[eng])
        best = {}
        for d in deps:
            if d is o:
                continue
            if d.dma is None and o.dma is None and d.eng == "pe" and o.eng == "pe":
                continue
            if d.dma is not None:
                key = ("dma", id(d.dma))
                if key not in best or best[key].dma_val < d.dma_val:
                    best[key] = d
            else:
                key = ("eng", d.eng)
                if key not in best or best[key].idx < d.idx:
                    best[key] = d
        o.deps = list(best.values())
        for b in reads:
            b.reads.append(o)
        for b in writes:
            b.last_write = o
            b.reads = []
        if dma is not None:
            dma.count += 16
            o.dma_val = dma.count
            self.dma_ops.append(o)
        self.q[eng].append(o)
        return o

    def barrier(self):
        lasts = []
        for e in ENGS:
            for o in reversed(self.q[e]):
                if o.dma is None:
                    lasts.append(o)
                    break
        dmas = list(self.dma_ops)
        self.dma_ops = []
        for e in ENGS:
            self.op(e, lambda h: h.nop(), extra_deps=[o for o in lasts if o.eng != e] + dmas)

    def finalize(self):
        needs = set()
        for e in ENGS:
            for o in self.q[e]:
                for d in o.deps:
                    if d.dma is None:
                        needs.add(id(d))
        for e in ENGS:
            k = 0
            for o in self.q[e]:
                if o.dma is None and id(o) in needs:
                    k += 1
                    o.sig = k
        for e in ENGS:
            seen = {}
            for o in self.q[e]:
                w = {}
                for d in o.deps:
                    if d.dma is not None:
                        key = ("dma", id(d.dma))
                        val = d.dma_val
                        ref = d.dma
                    else:
                        key = ("eng", d.eng)
                        val = d.sig
                        ref = d.eng
                    if seen.get(key, 0) >= val:
                        continue
                    if key not in w or w[key][1] < val:
                        w[key] = (ref, val)
                for key, (ref, val) in w.items():
                    seen[key] = val
                o.waits = list(w.values())

    def emit(self, esem, h, e):
        for o in self.q[e]:
            for ref, val in o.waits:
                if isinstance(ref, DmaSem):
                    h.wait_ge(ref.sem, val)
                else:
                    h.wait_ge(esem[ref], val)
            ins = o.fn(h)
            if o.dma is not None:
                ins.then_inc(o.dma.sem, 16)
            elif o.sig is not None:
                ins.then_inc(esem[e], 1)


def _alibi_slopes():
    def pow2(n):
        start = 2.0 ** (-8.0 / n)
        return [start ** (i + 1) for i in range(n)]
    s = pow2(8) + pow2(16)[0::2][:4]
    return np.array(sorted(s, reverse=True), dtype=np.float64).reshape(3, 4)


def _mask_tables(first_in_seq):
    sl = _alibi_slopes()
    k = np.arange(128)[:, None].astype(np.float64)
    q = np.arange(128)[None, :].astype(np.float64)
    out = np.zeros((128, 2, 3, 4, 2, 128), np.float32)
    for g in range(3):
        for hh in range(4):
            s = sl[g, hh] * DIL[g]
            prev = np.where(k >= q, np.exp(-s * (q + 128 - k)), 0.0)
            cur = np.where(k <= q, np.exp(-s * (q - k)), 0.0)
            out[:, 0, g, hh, 0] = prev
            out[:, 0, g, hh, 1] = cur
            out[:, 1, g, hh, 0] = 0.0 if first_in_seq else prev
            out[:, 1, g, hh, 1] = cur
    return out.reshape(128, 2 * 3 * 4 * 256)


def build_nc(debug=False):
    nc = bass.Bass("TRN2", target_bir_lowering=False)

    def din(name, shape):
        return nc.dram_tensor(name, list(shape), F32, kind="ExternalInput").ap()

    xT_d = din("xT", [D, HALO + NT])
    memT_d = din("memT", [D, 256])
    cpack_d = din("cpack", [128, 64 + 4 * 512])
    cmask_d = din("cmask", [128, 24 * 256])
    w_in_d = din("w_in", [D, IN_W])
    w_ba_d = din("w_ba", [256, D])
    w_bs_d = din("w_bs", [512, D])
    w_out_d = din("w_out", [D, D])
    w_qc_d = din("w_qc", [D, 512])
    w_kvc_d = din("w_kvc", [D, 1024])
    w_oc_d = din("w_oc", [512, D])
    w_gu_d = din("w_gu", [D, 2 * D_FF])
    w_dn_d = din("w_dn", [D_FF, D])
    outT_d = nc.dram_tensor("outT", [D, NT], F32, kind="ExternalOutput").ap()
    dbg_d = {}
    if debug:
        for nm, shp in (("d_q", [128, 6 * 2048]), ("d_y", [128, 2 * 2048]), ("d_h1", [128, 8 * 2048]),
                        ("d_h2", [128, 8 * 2048])):
            dbg_d[nm] = nc.dram_tensor(nm, shp, F32, kind="ExternalOutput").ap()

    S = Sched()
    st = ExitStack()
    with st:
        ARENA = 52992
        arena = st.enter_context(nc.sbuf_tensor("arena", [128, ARENA], F32))
        psum = [st.enter_context(nc.psum_tensor("ps%d" % i, [128, 512], F32)) for i in range(8)]
        pbuf = [Buf("ps%d" % i) for i in range(8)]
        esem = {e: st.enter_context(nc.semaphore("s_" + e)) for e in ENGS}
        pcount = [0]

        def nbank():
            i = pcount[0] % 8
            pcount[0] += 1
            assert pbuf[i].last_write is None or len(pbuf[i].reads) > 0, "PSUM bank %d re-used before being read" % i
            return psum[i][:, :], pbuf[i]

        def fv(off, n):
            assert off % 4 == 0 and off // 4 + n <= ARENA, (off, n)
            return arena[:, off // 4: off // 4 + n]

        def bv(off, n):
            assert off % 4 == 0 and n % 2 == 0 and off // 4 + n // 2 <= ARENA, (off, n)
            return arena[:, off // 4: off // 4 + n // 2].bitcast(BF16)

        KB = 1024
        o = 0
        cpack = fv(o, 64 + 4 * 512)
        cvec = fv(o, 56); o += 256
        gsgu = fv(o, 512); o += 2 * KB
        bsgu = fv(o, 512); o += 2 * KB
        bsg16 = bv(o, 512)
        wspf = fv(o, 512); o += 2 * KB
        trilf = fv(o, 512); o += 2 * KB
        ones = bv(o, 128); o += 256
        wspT = bv(o, 512); o += 1 * KB
        cmask = bv(o, 24 * 256); o += 12 * KB
        kmT = bv(o, 4 * 256).rearrange("p (h m) -> p h m", h=4); o += 2 * KB
        vm = bv(o, 2 * 512).rearrange("p (c f) -> p c f", c=2); o += 2 * KB
        ssq = fv(o, 8); o += 32
        rsv = fv(o, 8); o += 32
        assert o <= 26 * KB, o
        o = 26 * KB
        RING_SLOT = 8 * KB
        ring = [bv(o + i * RING_SLOT, 4096) for i in range(4)]
        ring_buf = [Buf("ring%d" % i) for i in range(4)]
        ring_sem = [S.new_dmasem("ring%d" % i) for i in range(4)]
        o += 4 * RING_SLOT
        YT_OFF = o
        yT = bv(o, 2 * 2048).rearrange("p (c t) -> p c t", c=2); o += 8 * KB
        P0 = o
        B_const = Buf("const")
        B_cmask = Buf("cmask")
        B_yT = Buf("yT")

        for d in S.dmasems:
            pass
        misc_sems = {}

        def dsem(name):
            if name not in misc_sems:
                misc_sems[name] = S.new_dmasem(name)
            return misc_sems[name]

        rcount = [0]

        def load_w(src, kch, ncols):
            assert kch * ncols <= 4096
            i = rcount[0] % 4
            rcount[0] += 1
            view = ring[i][:, 0:kch * ncols].rearrange("p (c f) -> p c f", c=kch)
            srcv = src.rearrange("(c p) f -> p c f", p=128)
            S.op("pool", lambda h, v=view, s=srcv: h.dma_start(out=v, in_=s), writes=[ring_buf[i]], dma=ring_sem[i])
            return view, ring_buf[i]

        def load_w_multi(pieces):
            i = rcount[0] % 4
            rcount[0] += 1
            off = 0
            views = []
            for src, kch, ncols in pieces:
                view = ring[i][:, off:off + kch * ncols].rearrange("p (c f) -> p c f", c=kch)
                off += kch * ncols
                assert off <= 4096
                srcv = src.rearrange("(c p) f -> p c f", p=128)
                S.op("pool", lambda h, v=view, s_=srcv: h.dma_start(out=v, in_=s_), writes=[ring_buf[i]], dma=ring_sem[i])
                views.append(view)
            return views, ring_buf[i]

        def dump(name, view):
            if not debug:
                return
            dt = view.dtype
            dr = nc.dram_tensor(name, list(view.shape), dt, kind="ExternalOutput").ap()
            S.barrier()
            S.op("sp", lambda h: h.dma_start(out=dr, in_=view), dma=dsem("dbg"))
            S.barrier()

        def mm(out, lhsT, rhs, start, stop, reads, wbuf):
            S.op("pe", lambda h: h.matmul(out, lhsT=lhsT, rhs=rhs, start=start, stop=stop), reads=reads, writes=[wbuf])

        def act(out, in_, func, reads, writes, **kw):
            S.op("act", lambda h: h.activation(out=out, in_=in_, func=func, **kw), reads=reads, writes=writes)

        def dve(fn, reads, writes):
            S.op("dve", fn, reads=reads, writes=writes)

        ev_rr = [0]

        def evac_copy(out, in_, reads, writes):
            ev_rr[0] += 1
            if ev_rr[0] % 2:
                S.op("act", lambda h: h.activation(out=out, in_=in_, func=AF.Copy), reads=reads, writes=writes)
            else:
                S.op("dve", lambda h: h.tensor_copy(out=out, in_=in_), reads=reads, writes=writes)

        cs = dsem("const")
        B_ones = Buf("ones")
        S.op("dve", lambda h: h.memset(ones, 1.0), writes=[B_ones])
        S.op("dve", lambda h: h.memset(ssq, 0.0), writes=[B_ones])

        def load_consts():
            S.op("sp", lambda h: h.dma_start(out=cpack, in_=cpack_d), writes=[B_const], dma=cs)
        consts_loaded = [False]
        G_MIX, G_CROSS, G_MEM, G_FFN, G_FINAL, B_GATE = 0, 8, 16, 24, 32, 40

        def rmsnorm_fm(src_all, src_fn, src_bufs, ntok, gcol, dst_fn, dst_bufs, sq, sq_buf, rs, rs_buf, nch=8, dim=D, pool_chunks=()):
            act(sq[:, :, 0:ntok], src_all, AF.Square, src_bufs, [sq_buf])
            ps, pb = nbank()
            for c in range(nch):
                mm(ps[:, 0:ntok], ones, sq[:, c, 0:ntok], c == 0, c == nch - 1, [sq_buf, B_ones], pb)
            act(rs[:, 0:ntok], ps[:, 0:ntok], AF.Ln, [pb], [rs_buf], scale=1.0 / dim, bias=EPS)
            act(rs[:, 0:ntok], rs[:, 0:ntok], AF.Exp, [rs_buf], [rs_buf], scale=-0.5)
            if pool_chunks:
                npc = len(pool_chunks)
                nd = nch - npc
                rb = rs[:, 0:ntok].unsqueeze(1)
                dst_all = dst_fn(None)
                S.op("dve", lambda h: h.tensor_tensor(out=dst_all[:, 0:nd, :], in0=src_all[:, 0:nd, :],
                                                      in1=rb.broadcast_to([128, nd, ntok]), op=ALU.mult),
                     reads=list(src_bufs) + [rs_buf], writes=dst_bufs)
                S.op("pool", lambda h: h.tensor_tensor(out=dst_all[:, nd:nch, :], in0=src_all[:, nd:nch, :],
                                                       in1=rb.broadcast_to([128, npc, ntok]), op=ALU.mult),
                     reads=list(src_bufs) + [rs_buf], writes=dst_bufs)
            else:
                for c in range(nch):
                    dve(lambda h, c=c: h.scalar_tensor_tensor(out=dst_fn(c), in0=src_fn(c), scalar=cvec[:, gcol + c:gcol + c + 1],
                                                              in1=rs[:, 0:ntok], op0=ALU.mult, op1=ALU.mult),
                        list(src_bufs) + [rs_buf, B_const], dst_bufs(c) if callable(dst_bufs) else dst_bufs)

        o = P0
        QT_OFF = o
        qT = bv(o, 6 * 2048).rearrange("p (c t) -> p c t", c=6); o += 24 * KB
        kT = []
        for g in range(3):
            L = GH[g] + NT
            kT.append(bv(o, 2 * L).rearrange("p (c t) -> p c t", c=2)); o += 4 * L
        NBLK = (17, 20, 32)
        Vt = []
        for g in range(3):
            Vt.append(bv(o, NBLK[g] * 256).rearrange("p (b f) -> p b f", b=NBLK[g])); o += NBLK[g] * 512
        aTf = bv(o, 8 * 2048).rearrange("p (c t) -> p c t", c=8); ACC_OFF = o; o += 32 * KB
        XS_OFF = o
        rsb = [fv(o + i * KB, 256) for i in range(4)]; o += 4 * KB
        XS3_OFF = o; o += 8 * KB
        SQ2_OFF = o; o += 4 * KB
        assert o <= ARENA * 4, o
        XT = 256
        xs = [fv(off_, 8 * XT).rearrange("p (c t) -> p c t", c=8) for off_ in (QT_OFF, QT_OFF + 8 * KB, YT_OFF, XS3_OFF)]
        sqs = [bv(off_, 8 * XT).rearrange("p (c t) -> p c t", c=8) for off_ in (QT_OFF + 16 * KB, QT_OFF + 20 * KB, SQ2_OFF)]
        B_qT = [Buf("qT%d" % c) for c in range(6)]
        B_kT = [[Buf("kT%d_%d" % (g, c)) for c in range(2)] for g in range(3)]
        B_V = [Buf("V%d" % g) for g in range(3)]
        B_aT = [Buf("aT%d" % t) for t in range(4)]
        B_xs = [Buf("xs%d" % i) for i in range(4)]
        B_sqs = [Buf("sqs%d" % i) for i in range(3)]
        B_rs = [Buf("rs%d" % i) for i in range(4)]
        xs_sem = [dsem("xs%d" % i) for i in range(4)]
        xTv = xT_d.rearrange("(c p) t -> p c t", p=128)

        def load_w_g(src, ncols):
            wv, wb = load_w(src, 8, ncols)
            for c in range(8):
                dve(lambda h, c=c, wv=wv: h.tensor_scalar(out=wv[:, c, :], in0=wv[:, c, :], scalar1=cvec[:, G_MIX + c:G_MIX + c + 1],
                                                          scalar2=None, op0=ALU.mult), [wb, B_const], [wb])
            return wv, wb

        def kproj(sb, t, wblk):
            for fc in range(6, 12):
                g, c2 = divmod(fc - 6, 2)
                if sb == 1:
                    tok0, ntok = t * 512, 512
                else:
                    lo = max(t * 512, 2048 - GH[g])
                    if lo >= (t + 1) * 512:
                        continue
                    tok0, ntok = lo, (t + 1) * 512 - lo
                wv, wb = wblk[fc // 4]
                fo = (fc % 4) * 128
                ps, pb = nbank()
                for c in range(8):
                    mm(ps[:, 0:ntok], wv[:, c, fo:fo + 128], aTf[:, c, tok0:tok0 + ntok],
                       c == 0, c == 7, [wb, B_aT[t]], pb)
                dst0 = (tok0 - (2048 - GH[g])) if sb == 0 else GH[g] + tok0
                evac_copy(kT[g][:, c2, dst0:dst0 + ntok], ps[:, 0:ntok], [pb], [B_kT[g][c2]])

        load_consts()
        dve(lambda h: h.tensor_tensor(out=wspT, in0=wspf, in1=trilf, op=ALU.mult), [B_const], [B_const])
        dve(lambda h: h.tensor_copy(out=bsg16, in_=bsgu), [B_const], [B_const])
        xcount = [0]
        for sb in range(2):
            wblk = {}
            for b in ((0, 1, 2) if sb == 1 else (1, 2)):
                wblk[b] = load_w_g(w_in_d[:, b * 512:(b + 1) * 512], 512)
            NXT = 2048 // XT
            pend = {}

            def stA(j):
                i = xcount[0] % 4
                i3 = xcount[0] % 3
                xcount[0] += 1
                t0 = sb * 2048 + j * XT
                S.op("sp", lambda h, i=i, t0=t0: h.dma_start(out=xs[i], in_=xTv[:, :, t0:t0 + XT]),
                     writes=[B_xs[i]], dma=xs_sem[i])
                act(sqs[i3][:, :, 0:XT], xs[i], AF.Square, [B_xs[i]], [B_sqs[i3]])
                ps, pb = nbank()
                for c in range(8):
                    mm(ps[:, 0:XT], ones, sqs[i3][:, c, 0:XT], c == 0, c == 7, [B_sqs[i3], B_ones], pb)
                pend[j] = (i, ps, pb)

            def stB(j):
                i, ps, pb = pend.pop(j)
                t = (j * XT) // 512
                rs = rsb[i]
                act(rs[:, 0:XT], ps[:, 0:XT], AF.Ln, [pb], [B_rs[i]], scale=1.0 / D, bias=EPS)
                act(rs[:, 0:XT], rs[:, 0:XT], AF.Exp, [B_rs[i]], [B_rs[i]], scale=-0.5)
                rb = rs[:, 0:XT].unsqueeze(1)
                dst = aTf[:, :, j * XT:(j + 1) * XT]
                S.op("dve", lambda h: h.tensor_tensor(out=dst[:, 0:5, :], in0=xs[i][:, 0:5, :],
                                                      in1=rb.broadcast_to([128, 5, XT]), op=ALU.mult),
                     reads=[B_xs[i], B_rs[i]], writes=[B_aT[t]])
                S.op("pool", lambda h: h.tensor_tensor(out=dst[:, 5:8, :], in0=xs[i][:, 5:8, :],
                                                       in1=rb.broadcast_to([128, 3, XT]), op=ALU.mult),
                     reads=[B_xs[i], B_rs[i]], writes=[B_aT[t]])

            stA(0)
            for j in range(NXT):
                if j + 1 < NXT:
                    stA(j + 1)
                stB(j)
                if (j + 1) % (512 // XT) == 0:
                    t = (j + 1) // (512 // XT) - 1
                    if t >= 1:
                        kproj(sb, t - 1, wblk)
            kproj(sb, 3, wblk)
            if sb == 1:
                for t in range(4):
                    for fc in range(6):
                        wv, wb = wblk[fc // 4]
                        fo = (fc % 4) * 128
                        ps, pb = nbank()
                        for c in range(8):
                            mm(ps, wv[:, c, fo:fo + 128], aTf[:, c, t * 512:(t + 1) * 512], c == 0, c == 7, [wb] + B_aT, pb)
                        evac_copy(qT[:, fc, t * 512:(t + 1) * 512], ps, [pb], [B_qT[fc]])
            for g in range(3):
                wv, wb = load_w_g(w_in_d[:, 1536 + g * 256:1536 + (g + 1) * 256], 256)
                blocks = []
                if g == 0:
                    if sb == 0:
                        blocks.append((0, slice(1920, 2048)))
                    else:
                        blocks += [(1 + n, slice(n * 128, (n + 1) * 128)) for n in range(16)]
                elif g == 1:
                    if sb == 0:
                        blocks += [(r, slice(1536 + r, 2048, 4)) for r in range(4)]
                    else:
                        blocks += [(4 + n1 * 4 + r, slice(512 * n1 + r, 512 * (n1 + 1), 4)) for n1 in range(4) for r in range(4)]
                else:
                    blocks += [(sb * 16 + r, slice(r, 2048, 16)) for r in range(16)]
                for p in range(0, len(blocks), 2):
                    grp = blocks[p:p + 2]
                    ps, pb = nbank()
                    for bi, (blk, sl) in enumerate(grp):
                        for c in range(8):
                            mm(ps[:, bi * 256:(bi + 1) * 256], aTf[:, c, sl], wv[:, c, :], c == 0, c == 7, [wb] + B_aT, pb)
                    b0 = grp[0][0]
                    n = len(grp)
                    evac_copy(Vt[g][:, b0:b0 + n, :], ps[:, 0:n * 256].rearrange("p (b f) -> p b f", b=n), [pb], [B_V[g]])

        S.op("pool", lambda h: h.dma_start(out=cmask, in_=cmask_d), writes=[B_cmask], dma=dsem("cmask"))
        if debug:
            S.barrier()
            dq = dsem("dbg")
            stg = fv(XS_OFF, 2048)
            B_stg = Buf("stg")
            for c in range(6):
                dve(lambda h, c=c: h.tensor_copy(out=stg, in_=qT[:, c, :]), [B_qT[c]], [B_stg])
                S.op("sp", lambda h, c=c: h.dma_start(out=dbg_d["d_q"][:, c * 2048:(c + 1) * 2048], in_=stg),
                     reads=[B_stg], dma=dq)
        dump("d_cmask", cmask)
        dump("d_kT0", kT[0].rearrange("p c t -> p (c t)"))
        dump("d_V0", Vt[0].rearrange("p b f -> p (b f)"))
        S.barrier()

        acc = fv(ACC_OFF, 2 * 2 * 2048).rearrange("p (c n t) -> p c n t", c=2, n=2)
        o = QT_OFF
        o = XS_OFF
        NEB = 3
        ET = [[bv(o + (2 * i + hh) * KB, 512) for hh in range(2)] for i in range(NEB)]; o += 2 * NEB * KB
        EM = [[bv(o + (2 * i + hh) * KB, 512) for hh in range(2)] for i in range(NEB)]; o += 2 * NEB * KB
        assert o <= ARENA * 4, o
        B_ET = [[ScrBuf("ET%d%d" % (i, hh)) for hh in range(2)] for i in range(NEB)]
        B_EM = [[ScrBuf("EM%d%d" % (i, hh)) for hh in range(2)] for i in range(NEB)]
        B_acc = [ScrBuf("acc%d" % c) for c in range(2)]
        cmv = cmask.rearrange("p (v g h x) -> p v g h x", v=2, g=3, h=4)

        def qsl(g, n):
            if g == 0:
                return slice(n * 128, (n + 1) * 128)
            if g == 1:
                n1, r = divmod(n, 4)
                return slice(512 * n1 + r, 512 * (n1 + 1), 4)
            return slice(n, 2048, 16)

        def ksl(g, n, kb):
            s_ = qsl(g, n)
            off = GH[g] if kb == 1 else GH[g] - 128 * DIL[g]
            return slice(s_.start + off, s_.stop + off, s_.step)

        def vblk(g, n, kb):
            if g == 0:
                return n + kb
            if g == 1:
                n1, r = divmod(n, 4)
                return (n1 + kb) * 4 + r
            return kb * 16 + n

        def is_halo(g, n):
            return (g == 0 and n == 0) or (g == 1 and n < 4) or g == 2

        iters = [(g, hp, npair) for g in range(3) for hp in range(2) for npair in range(8)]
        NI = len(iters)
        sps_of = {}

        def st_S(i):
            g, hp, npair = iters[i]
            ch = 2 * g + hp
            nn = (2 * npair, 2 * npair + 1)
            sps = []
            for hh in range(2):
                bk = (2 * i + hh) % 6
                ps, pb = psum[bk][:, :], pbuf[bk]
                sps.append((ps, pb))
                pr = slice(hh * 64, (hh + 1) * 64)
                for qi, n in enumerate(nn):
                    for kb in range(2):
                        col = (qi * 2 + kb) * 128
                        mm(ps[:, col:col + 128], kT[g][pr, hp, ksl(g, n, kb)], qT[pr, ch, qsl(g, n)],
                           True, True, [B_kT[g][hp], B_qT[ch]], pb)
            sps_of[i] = sps

        def st_E(i):
            g, hp, npair = iters[i]
            nn = (2 * npair, 2 * npair + 1)
            bi = i % NEB
            for hh in range(2):
                ps, pb = sps_of[i][hh]
                act(ET[bi][hh], ps, AF.Exp, [pb], [B_ET[bi][hh]], scale=0.125)
                vs = [1 if is_halo(g, n) else 0 for n in nn]
                eng = "dve" if hh == 0 else "pool"
                if vs[0] == vs[1]:
                    S.op(eng, lambda h, bi=bi, hh=hh, v=vs[0], g=g, hp=hp: h.tensor_tensor(
                        out=EM[bi][hh].rearrange("p (q x) -> p q x", q=2), in0=ET[bi][hh].rearrange("p (q x) -> p q x", q=2),
                        in1=cmv[:, v, g, 2 * hp + hh, :].unsqueeze(1).broadcast_to([128, 2, 256]), op=ALU.mult),
                        reads=[B_ET[bi][hh], B_cmask], writes=[B_EM[bi][hh]])
                else:
                    for qi, n in enumerate(nn):
                        S.op(eng, lambda h, bi=bi, hh=hh, qi=qi, v=vs[qi], g=g, hp=hp: h.tensor_tensor(
                            out=EM[bi][hh][:, qi * 256:(qi + 1) * 256], in0=ET[bi][hh][:, qi * 256:(qi + 1) * 256],
                            in1=cmv[:, v, g, 2 * hp + hh, :], op=ALU.mult),
                            reads=[B_ET[bi][hh], B_cmask], writes=[B_EM[bi][hh]])

        def st_P(i):
            g, hp, npair = iters[i]
            nn = (2 * npair, 2 * npair + 1)
            bi = i % NEB
            bk = 6 + (i % 2)
            ps, pb = psum[bk][:, :], pbuf[bk]
            for hh in range(2):
                pr = slice(hh * 64, (hh + 1) * 64)
                hcol = slice((2 * hp + hh) * 64, (2 * hp + hh + 1) * 64)
                for qi, n in enumerate(nn):
                    for kb in range(2):
                        e = EM[bi][hh][:, (qi * 2 + kb) * 128:(qi * 2 + kb + 1) * 128]
                        mm(ps[pr, qi * 128:(qi + 1) * 128], Vt[g][:, vblk(g, n, kb), hcol], e,
                           kb == 0, kb == 1, [B_V[g], B_EM[bi][hh]], pb)
                    for kb in range(2):
                        e = EM[bi][hh][:, (qi * 2 + kb) * 128:(qi * 2 + kb + 1) * 128]
                        mm(ps[pr, 256 + qi * 128:256 + (qi + 1) * 128], ones[:, 0:64], e,
                           kb == 0, kb == 1, [B_ones, B_EM[bi][hh]], pb)
            n0 = nn[0]
            if g == 0:
                dst = acc[:, hp, :, n0 * 128:(n0 + 2) * 128].rearrange("p n (q i) -> p n q i", q=2)
            elif g == 1:
                n1, r = divmod(n0, 4)
                dst = acc[:, hp, :, 512 * n1:512 * (n1 + 1)].rearrange("p n (i r) -> p n r i", r=4)[:, :, r:r + 2, :]
            else:
                dst = acc[:, hp, :, :].rearrange("p n (i r) -> p n r i", r=16)[:, :, n0:n0 + 2, :]
            src = ps.rearrange("p (n q i) -> p n q i", n=2, q=2)
            if g == 0:
                act(dst, src, AF.Copy, [pb], [B_acc[hp]])
            else:
                dve(lambda h, dst=dst, src=src: h.tensor_tensor(out=dst, in0=src, in1=dst, op=ALU.add),
                    [pb, B_acc[hp]], [B_acc[hp]])

        st_S(0)
        st_S(1)
        st_E(0)
        for i in range(NI):
            if i + 2 < NI:
                st_S(i + 2)
            if i + 1 < NI:
                st_E(i + 1)
            st_P(i)
        dump("d_acc", acc.rearrange("p c n t -> p (c n t)"))
        for hp in range(2):
            act(acc[:, hp, 1, :], acc[:, hp, 1, :], AF.Ln, [B_acc[hp]], [B_acc[hp]])
            act(acc[:, hp, 1, :], acc[:, hp, 1, :], AF.Exp, [B_acc[hp]], [B_acc[hp]], scale=-1.0)
            dve(lambda h, hp=hp: h.tensor_tensor(out=yT[:, hp, :], in0=acc[:, hp, 0, :], in1=acc[:, hp, 1, :], op=ALU.mult),
                [B_acc[hp]], [B_yT])
        if debug:
            S.barrier()
            stg = fv(XS_OFF + 8 * KB, 2048)
            for c in range(2):
                dve(lambda h, c=c: h.tensor_copy(out=stg, in_=yT[:, c, :]), [B_yT], [B_stg])
                S.op("sp", lambda h, c=c: h.dma_start(out=dbg_d["d_y"][:, c * 2048:(c + 1) * 2048], in_=stg),
                     reads=[B_stg], dma=dq)
            S.barrier()

        o = P0
        hT = fv(o, 8 * 2048).rearrange("p (c t) -> p c t", c=8); o += 64 * KB
        HT = 1024
        nT = bv(o, 8 * HT).rearrange("p (c t) -> p c t", c=8); o += 16 * KB
        sq3 = bv(o, 8 * 512).rearrange("p (c t) -> p c t", c=8); o += 8 * KB
        rs3 = [fv(o + i * 2 * KB, 512) for i in range(2)]; o += 4 * KB
        SCR = o
        B_hT = [[Buf("hT%d_%d" % (c, t)) for t in range(4)] for c in range(8)]
        B_nT = [Buf("nT%d" % t) for t in range(2)]
        B_sq3 = Buf("sq3")
        B_rs3 = [Buf("rs3_%d" % i) for i in range(2)]
        hs = dsem("hT")
        p12_bufs = B_qT + [b_ for g_ in range(3) for b_ in B_kT[g_]] + B_V + B_aT
        hprev = None
        for t in range(4):
            hprev = S.op("sp", lambda h, t=t: h.dma_start(out=hT[:, :, t * 512:(t + 1) * 512],
                                                           in_=xTv[:, :, HALO + t * 512:HALO + (t + 1) * 512]),
                         writes=[B_hT[c][t] for c in range(8)] + p12_bufs, dma=hs,
                         extra_deps=([hprev] if (hprev is not None and t >= 1) else []))
        S.op("sp", lambda h: h.nop(), writes=[S.fence_buf])

        B_kv = Buf("kvmem")

        def kv_mem(o):
            memf = fv(o, 8 * 256).rearrange("p (c t) -> p c t", c=8); o += 8 * KB
            mnT = bv(o, 8 * 256).rearrange("p (c t) -> p c t", c=8); o += 4 * KB
            assert o <= ARENA * 4, o
            B_memf = ScrBuf("memf"); B_mnT = ScrBuf("mnT")
            S.op("sp", lambda h: h.dma_start(out=memf, in_=memT_d.rearrange("(c p) t -> p c t", p=128)),
                 writes=[B_memf], dma=dsem("memf"))
            rmsnorm_fm(memf, lambda c: memf[:, c, :], [B_memf], 256, G_MEM, lambda c: mnT[:, c, :], [B_mnT],
                       sq3, B_sq3, rs3[0], B_rs3[0])
            wv, wb = load_w(w_kvc_d[:, 0:512], 8, 512)
            for hd in range(4):
                ps, pb = nbank()
                for c in range(8):
                    mm(ps[:, 0:256], wv[:, c, hd * 128:(hd + 1) * 128], mnT[:, c, :], c == 0, c == 7, [wb, B_mnT], pb)
                evac_copy(kmT[:, hd, :], ps[:, 0:256], [pb], [B_kv])
            wv, wb = load_w(w_kvc_d[:, 512:1024], 8, 512)
            for mc in range(2):
                ps, pb = nbank()
                for c in range(8):
                    mm(ps, mnT[:, c, mc * 128:(mc + 1) * 128], wv[:, c, :], c == 0, c == 7, [wb, B_mnT], pb)
                evac_copy(vm[:, mc, :], ps, [pb], [B_kv])

        def norm_tile(H, tt, gcol):
            t = 2 * H + tt
            rmsnorm_fm(hT[:, :, t * 512:(t + 1) * 512], lambda c, t=t: hT[:, c, t * 512:(t + 1) * 512], [B_hT[c][t] for c in range(8)], 512, gcol,
                       lambda c, tt=tt: nT[:, c, tt * 512:(tt + 1) * 512], [B_nT[tt]],
                       sq3, B_sq3, rs3[tt], B_rs3[tt])

        def norm_half(H, gcol):
            for tt in range(2):
                norm_tile(H, tt, gcol)

        def resid_add(ps, pb, m, t):
            dve(lambda h: h.tensor_tensor(out=hT[:, m, t * 512:(t + 1) * 512], in0=ps, in1=hT[:, m, t * 512:(t + 1) * 512],
                                          op=ALU.add), [pb, B_hT[m][t]], [B_hT[m][t]])

        import os
        K_FENCE = os.environ.get("K_FENCE", "1") == "1"
        K_HOIST = os.environ.get("K_HOIST", "1") == "1"
        K_PIPE = os.environ.get("K_PIPE", "1") == "1"

        def fence():
            if K_FENCE:
                S.op("sp", lambda h: h.nop(), writes=[S.fence_buf])
            else:
                S.barrier()

        norm_half(0, G_MIX)
        for H in range(2):
            tok0 = H * HT
            o = SCR
            uT = bv(o, 4 * HT).rearrange("p (c t) -> p c t", c=4); o += 8 * KB
            ysT = bv(o, 4 * HT).rearrange("p (c t) -> p c t", c=4); o += 8 * KB
            vg = fv(o, 8 * 512).rearrange("p (b f) -> p b f", b=8)
            mgT = bv(o, 8 * HT).rearrange("p (c t) -> p c t", c=8); o += 16 * KB
            NVN = 4
            vn = [bv(o + i * KB, 512) for i in range(NVN)]; o += NVN * KB
            mxb = [fv(o + i * 2 * KB, 512) for i in range(2)]; o += 4 * KB
            gt = [bv(o + i * KB, 512) for i in range(4)]; o += 4 * KB
            t1b = [fv(o + i * 2 * KB, 512) for i in range(2)]; o += 4 * KB
            assert o <= ARENA * 4, o
            B_uT = ScrBuf("uT"); B_ysT = ScrBuf("ysT"); B_vg = ScrBuf("vg"); B_vn = [ScrBuf("vn%d" % i) for i in range(NVN)]
            B_mx = [ScrBuf("mx0"), ScrBuf("mx1")]; B_mg = [ScrBuf("mg%d" % t) for t in range(2)]
            B_gt = [ScrBuf("gt%d" % i) for i in range(4)]; B_t1 = [ScrBuf("t1_0"), ScrBuf("t1_1")]
            B_ssq = Buf("ssq")
            if H == 1 and not K_HOIST:
                norm_half(1, G_MIX)
            dve(lambda h: h.memset(ssq, 0.0), [], [B_ssq])
            wvu, wbu = load_w(w_in_d[:, 2304:2816], 8, 512)
            wvv, wbv = load_w(w_in_d[:, 2816:3328], 8, 512)
            for tt in range(2):
                for blk in range(4 * tt, 4 * tt + 4):
                    bo = (blk * 128) % 512
                    ps, pb = nbank()
                    for c in range(8):
                        mm(ps, nT[:, c, tt * 512 + bo:tt * 512 + bo + 128], wvv[:, c, :], c == 0, c == 7, [wbv, B_nT[tt]], pb)
                    act(vg[:, blk, :], ps, AF.Gelu, [pb], [B_vg])
                    act(t1b[blk % 2], vg[:, blk, :], AF.Square, [B_vg], [B_t1[blk % 2], B_ssq], accum_out=ssq[:, blk:blk + 1])
                for fc in range(4):
                    ps, pb = nbank()
                    for c in range(8):
                        mm(ps, wvu[:, c, fc * 128:(fc + 1) * 128], nT[:, c, tt * 512:(tt + 1) * 512], c == 0, c == 7,
                           [wbu, B_nT[tt]], pb)
                    act(uT[:, fc, tt * 512:(tt + 1) * 512], ps, AF.Gelu, [pb], [B_uT])
            act(rsv, ssq, AF.Ln, [B_ssq], [B_ssq], scale=1.0 / 512, bias=EPS)
            act(rsv, rsv, AF.Exp, [B_ssq], [B_ssq], scale=-0.5)
            for blk in range(8):
                i = blk % NVN
                dve(lambda h, blk=blk, i=i: h.scalar_tensor_tensor(out=vn[i], in0=vg[:, blk, :], scalar=rsv[:, blk:blk + 1],
                                                                    in1=gsgu, op0=ALU.mult, op1=ALU.mult),
                    [B_vg, B_ssq, B_const], [B_vn[i]])
                ps, pb = nbank()
                mm(ps, ones[0:1, 0:128], bsg16[0:1, :], True, False, [B_ones, B_const], pb)
                for gi in range(4):
                    mm(ps[:, gi * 128:(gi + 1) * 128], vn[i][:, gi * 128:(gi + 1) * 128], wspT[:, gi * 128:(gi + 1) * 128],
                       False, gi == 3, [B_vn[i], B_const], pb)
                dve(lambda h, ps=ps, blk=blk: h.tensor_tensor(
                    out=ysT[:, :, blk * 128:(blk + 1) * 128], in0=ps.rearrange("p (g t) -> p g t", g=4),
                    in1=uT[:, :, blk * 128:(blk + 1) * 128], op=ALU.mult), [pb, B_uT], [B_ysT])
            for m in range(8):
                fsl = slice(m * 128, (m + 1) * 128)
                (wg0, wg1, wba, wbs), wsb = load_w_multi([
                    (w_in_d[:, 3328 + m * 128:3328 + (m + 1) * 128], 8, 128),
                    (w_in_d[:, 4352 + m * 128:4352 + (m + 1) * 128], 8, 128),
                    (w_ba_d[:, fsl], 2, 128),
                    (w_bs_d[:, fsl], 4, 128)])
                for tt in range(2):
                    ts = slice(tt * 512, (tt + 1) * 512)
                    tsg = slice(tok0 + tt * 512, tok0 + (tt + 1) * 512)
                    gi0 = (2 * tt) % 4
                    for k, (wv_, boff) in enumerate(((wg0, 0), (wg1, 8))):
                        ps, pb = nbank()
                        for c in range(8):
                            mm(ps, wv_[:, c, :], nT[:, c, ts], c == 0, c == 7, [wsb, B_nT[tt]], pb)
                        act(gt[gi0 + k], ps, AF.Sigmoid, [pb, B_const], [B_gt[gi0 + k]],
                            bias=cvec[:, B_GATE + boff + m:B_GATE + boff + m + 1])
                    ps, pb = nbank()
                    for c in range(2):
                        mm(ps, wba[:, c, :], yT[:, c, tsg], c == 0, c == 1, [wsb, B_yT], pb)
                    dve(lambda h, ps=ps, i=tt, gi0=gi0: h.tensor_tensor(out=t1b[i], in0=ps, in1=gt[gi0], op=ALU.mult),
                        [pb, B_gt[gi0]], [B_t1[tt]])
                    ps, pb = nbank()
                    for c in range(4):
                        mm(ps, wbs[:, c, :], ysT[:, c, ts], c == 0, c == 3, [wsb, B_ysT], pb)
                    dve(lambda h, ps=ps, i=tt, gi0=gi0: h.tensor_tensor(out=mxb[i], in0=ps, in1=gt[gi0 + 1], op=ALU.mult),
                        [pb, B_gt[gi0 + 1]], [B_mx[tt]])
                    dve(lambda h, i=tt, m=m, ts=ts: h.tensor_tensor(out=mgT[:, m, ts], in0=t1b[i], in1=mxb[i], op=ALU.add),
                        [B_t1[tt], B_mx[tt]], [B_mg[tt], B_vg])
            wo = [load_w(w_out_d[:, mg * 512:(mg + 1) * 512], 8, 512) for mg in range(2)]
            for tt in range(2):
                for m in range(8):
                    wv, wb = wo[m // 4]
                    mc = m % 4
                    ps, pb = nbank()
                    for c in range(8):
                        mm(ps, wv[:, c, mc * 128:(mc + 1) * 128], mgT[:, c, tt * 512:(tt + 1) * 512], c == 0, c == 7,
                           [wb, B_mg[tt]], pb)
                    resid_add(ps, pb, m, 2 * H + tt)
            fence()
            o = SCR
            qcT = bv(o, 4 * HT).rearrange("p (c t) -> p c t", c=4); o += 8 * KB
            ocT = bv(o, 4 * HT).rearrange("p (c t) -> p c t", c=4); o += 8 * KB
            NEC = 3
            ec = [bv(o + i * 2 * KB, 1024).rearrange("p (c t) -> p c t", c=2) for i in range(NEC)]; o += NEC * 2 * KB
            rd = [fv(o + i * 2 * KB, 512) for i in range(2)]; o += 4 * KB
            B_qc = [ScrBuf("qcT0"), ScrBuf("qcT1")]; B_oc = [ScrBuf("ocT0"), ScrBuf("ocT1")]
            B_ec = [ScrBuf("ec%d" % i) for i in range(NEC)]; B_rd = [ScrBuf("rd0"), ScrBuf("rd1")]
            norm_half(H, G_CROSS)
            wv, wb = load_w(w_qc_d, 8, 512)
            for tt in range(2):
                for hd in range(4):
                    ps, pb = nbank()
                    for c in range(8):
                        mm(ps, wv[:, c, hd * 128:(hd + 1) * 128], nT[:, c, tt * 512:(tt + 1) * 512], c == 0, c == 7,
                           [wb, B_nT[tt]], pb)
                    evac_copy(qcT[:, hd, tt * 512:(tt + 1) * 512], ps, [pb], [B_qc[tt]])
                if H == 0 and tt == 0:
                    kv_mem(o)
            woc = [load_w(w_oc_d[:, mg * 512:(mg + 1) * 512], 4, 512) for mg in range(2)]
            cits = [(tt, hd) for tt in range(2) for hd in range(4)]
            csp = {}

            def c_S(i):
                tt, hd = cits[i]
                ts = slice(tt * 512, (tt + 1) * 512)
                lst = []
                for mc in range(2):
                    ps, pb = nbank()
                    mm(ps, kmT[:, hd, mc * 128:(mc + 1) * 128], qcT[:, hd, ts], True, True, [B_kv, B_qc[tt]], pb)
                    lst.append((ps, pb))
                csp[i] = lst

            def c_E(i):
                k = i % NEC
                for mc in range(2):
                    ps, pb = csp[i][mc]
                    act(ec[k][:, mc, :], ps, AF.Exp, [pb], [B_ec[k]], scale=1.0 / math.sqrt(128.0))

            def c_P(i):
                tt, hd = cits[i]
                ts = slice(tt * 512, (tt + 1) * 512)
                k = i % NEC
                j = i % 2
                psn, pbn = nbank()
                for mc in range(2):
                    mm(psn, vm[:, mc, hd * 128:(hd + 1) * 128], ec[k][:, mc, :], mc == 0, mc == 1, [B_kv, B_ec[k]], pbn)
                psd, pbd = nbank()
                for mc in range(2):
                    mm(psd, ones, ec[k][:, mc, :], mc == 0, mc == 1, [B_ones, B_ec[k]], pbd)
                act(rd[j], psd, AF.Ln, [pbd], [B_rd[j]])
                act(rd[j], rd[j], AF.Exp, [B_rd[j]], [B_rd[j]], scale=-1.0)
                dve(lambda h: h.tensor_tensor(out=ocT[:, hd, ts], in0=psn, in1=rd[j], op=ALU.mult),
                    [pbn, B_rd[j]], [B_oc[tt]])

            def c_O(tt):
                for m in range(8):
                    wv, wb = woc[m // 4]
                    mc = m % 4
                    ps, pb = nbank()
                    for c in range(4):
                        mm(ps, wv[:, c, mc * 128:(mc + 1) * 128], ocT[:, c, tt * 512:(tt + 1) * 512], c == 0, c == 3,
                           [wb, B_oc[tt]], pb)
                    resid_add(ps, pb, m, 2 * H + tt)

            if K_PIPE:
                for tt in range(2):
                    b0 = 4 * tt
                    c_S(b0); c_S(b0 + 1); c_E(b0)
                    for i in range(b0, b0 + 4):
                        if i + 2 < b0 + 4:
                            c_S(i + 2)
                        if i + 1 < b0 + 4:
                            c_E(i + 1)
                        c_P(i)
                    c_O(tt)
            else:
                for i in range(8):
                    c_S(i); c_E(i); c_P(i)
                    if i == 3:
                        c_O(0)
                c_O(1)
            fence()
            o = SCR
            hid = bv(o, 22 * HT).rearrange("p (c t) -> p c t", c=22); o += 44 * KB
            sg = [bv(o + i * KB, 512) for i in range(2)]; o += 2 * KB
            assert o <= ARENA * 4, o
            B_hid = [ScrBuf("hid%d" % t) for t in range(2)]
            B_sg = [ScrBuf("sg0"), ScrBuf("sg1")]
            norm_half(H, G_FFN)
            its = 0
            for jb in range(0, 22, 4):
                nj = min(4, 22 - jb)
                wg, wgb = load_w(w_gu_d[:, jb * 128:(jb + nj) * 128], 8, nj * 128)
                wu, wub = load_w(w_gu_d[:, D_FF + jb * 128:D_FF + (jb + nj) * 128], 8, nj * 128)
                order = [(jj, tt) for jj in range(nj) for tt in range(2)] if jb > 0 else \
                        [(jj, tt) for tt in range(2) for jj in range(nj)]
                for jj, tt in order:
                    if True:
                        j = jb + jj
                        fs = slice(jj * 128, (jj + 1) * 128)
                        ts = slice(tt * 512, (tt + 1) * 512)
                        i = its % 2
                        its += 1
                        psg, pbg = nbank()
                        for c in range(8):
                            mm(psg, wg[:, c, fs], nT[:, c, ts], c == 0, c == 7, [wgb, B_nT[tt]], pbg)
                        psu, pbu = nbank()
                        for c in range(8):
                            mm(psu, wu[:, c, fs], nT[:, c, ts], c == 0, c == 7, [wub, B_nT[tt]], pbu)
                        act(sg[i], psg, AF.Silu, [pbg], [B_sg[i]])
                        dve(lambda h, i=i, psu=psu, j=j, ts=ts: h.tensor_tensor(out=hid[:, j, ts], in0=psu, in1=sg[i], op=ALU.mult),
                            [pbu, B_sg[i]], [B_hid[tt]])
            def down_group(m, wv, wb, tt):
                ps, pb = nbank()
                for j in range(22):
                    mm(ps, wv[:, j, :], hid[:, j, tt * 512:(tt + 1) * 512], j == 0, j == 21, [wb, B_hid[tt]], pb)
                resid_add(ps, pb, m, 2 * H + tt)

            for m in range(4):
                wv, wb = load_w(w_dn_d[:, m * 128:(m + 1) * 128], 22, 128)
                for tt in range(2):
                    down_group(m, wv, wb, tt)
                if H == 0 and K_HOIST and m < 2:
                    norm_tile(1, m, G_MIX)
            wd = [load_w(w_dn_d[:, m * 128:(m + 1) * 128], 22, 128) for m in range(4, 8)]
            for tt in range(2):
                for mi, m in enumerate(range(4, 8)):
                    down_group(m, wd[mi][0], wd[mi][1], tt)
            fence()
            o = SCR
            ost = [fv(o + i * 16 * KB, 8 * 512).rearrange("p (c t) -> p c t", c=8) for i in range(2)]; o += 32 * KB
            B_ost = [[ScrBuf("ost%d_%d" % (i, k)) for k in range(2)] for i in range(2)]
            osem = [dsem("ost0"), dsem("ost1")]
            outv = outT_d.rearrange("(c p) t -> p c t", p=128)
            for tt in range(2):
                t = 2 * H + tt
                rmsnorm_fm(hT[:, :, t * 512:(t + 1) * 512], lambda c, t=t: hT[:, c, t * 512:(t + 1) * 512], [B_hT[c][t] for c in range(8)], 512, G_FINAL,
                           lambda c, tt=tt: ost[tt][:, c, :], (lambda c, tt=tt: [B_ost[tt][c // 4]]),
                           sq3 if (tt == 0 or H == 0) else nT[:, :, 0:512], B_sq3 if (tt == 0 or H == 0) else B_nT[0],
                           rs3[tt], B_rs3[tt])
                for k in range(2):
                    S.op("sp", lambda h, tt=tt, t=t, k=k: h.dma_start(out=outv[:, 4 * k:4 * k + 4, t * 512:(t + 1) * 512],
                                                                      in_=ost[tt][:, 4 * k:4 * k + 4, :]),
                         reads=[B_ost[tt][k]], dma=osem[tt])
            fence()
        S.barrier()

        for d in S.dmasems:
            d.sem = st.enter_context(nc.semaphore("d_" + d.name))
        S.finalize()
        with nc.Block() as block:
            @block.tensor
            def _(h):
                S.emit(esem, h, "pe")

            @block.scalar
            def _(h):
                S.emit(esem, h, "act")

            @block.vector
            def _(h):
                S.emit(esem, h, "dve")

            @block.gpsimd
            def _(h):
                S.emit(esem, h, "pool")

            @block.sync
            def _(h):
                S.emit(esem, h, "sp")
    return nc


def make_in_maps(inputs):
    f = lambda a: np.ascontiguousarray(np.asarray(a, dtype=np.float32))
    x = f(inputs["x"]); mem = f(inputs["mem"])
    vec8 = lambda v: np.asarray(v, np.float32).reshape(8, 128).T
    cvec = np.concatenate([vec8(inputs["g_mix"][0]), vec8(inputs["g_cross"][0]), vec8(inputs["g_mem"][0]),
                           vec8(inputs["g_ffn"][0]), vec8(inputs["g_final"]),
                           np.asarray(inputs["b_gate"][0], np.float32).reshape(16, 128).T], axis=1)
    gsgu_b = np.broadcast_to(np.asarray(inputs["g_sgu"][0], np.float32)[None, :], (128, 512))
    bsgu_b = np.broadcast_to(np.asarray(inputs["b_sgu_spatial"][0], np.float32).reshape(1, 512), (128, 512))
    wsp = np.asarray(inputs["w_sgu_spatial"][0], np.float32)
    wspT = np.transpose(wsp, (2, 0, 1)).reshape(128, 512)
    s_i = np.arange(128)[:, None]; t_i = np.arange(128)[None, :]
    trilm = np.tile((s_i <= t_i).astype(np.float32), (1, 4))
    shared = {
        "cpack": f(np.concatenate([cvec, np.zeros((128, 8), np.float32), gsgu_b, bsgu_b, wspT, trilm], axis=1)),
        "w_in": f(inputs["w_in"][0]), "w_ba": f(inputs["w_branch_attn"][0]), "w_bs": f(inputs["w_branch_sgu"][0]),
        "w_out": f(inputs["w_out"][0]), "w_qc": f(inputs["w_q_cross"][0]), "w_kvc": f(inputs["w_kv_cross"][0]),
        "w_oc": f(inputs["w_o_cross"][0]), "w_gu": f(inputs["w_gate_up"][0]), "w_dn": f(inputs["w_down"][0]),
    }
    masks = [f(_mask_tables(True)), f(_mask_tables(False))]
    maps = []
    for core in range(8):
        b, part = divmod(core, 4)
        t0 = part * NT
        xT = np.zeros((D, HALO + NT), np.float32)
        if part > 0:
            xT[:, 0:HALO] = x[b, t0 - HALO:t0, :].T
        xT[:, HALO:] = x[b, t0:t0 + NT, :].T
        m = dict(shared)
        m["xT"] = xT
        m["memT"] = f(mem[b].T)
        m["cmask"] = masks[0] if part == 0 else masks[1]
        maps.append(m)
    return maps


def kernel(**inputs):
    nc = build_nc()
    maps = make_in_maps(inputs)
    res = run_bass_kernel_spmd(nc, maps, core_ids=list(range(8)))
    out = np.zeros((2, 8192, D), np.float32)
    for core in range(8):
        b, part = divmod(core, 4)
        out[b, part * NT:(part + 1) * NT, :] = res.results[core]["outT"].T
    return out
```
